# Optimizing a Trainium2 kernel written in Bass

```python
import jax, jax.numpy as jnp
from jax import lax
import numpy as np

D_MODEL = 1024
BATCH = 2
SEQ = 8192
DEPTH = 2

N_META = 16
CHUNK = 64
PAD = CHUNK - N_META
D_FF = 2816
EPS = 1e-6
N_MIXERS = 2
N_RET = (DEPTH + 1) // 2
N_GLA = DEPTH // 2

RET_HEADS = 4
RET_DK = D_MODEL // RET_HEADS
RET_DV = 2 * D_MODEL // RET_HEADS
RET_IN = 2 * D_MODEL + 4 * D_MODEL
ROPE_BASE = 10000.0

GLA_HEADS = 4
GLA_DK = D_MODEL // 2 // GLA_HEADS
GLA_DV = D_MODEL // GLA_HEADS
GLA_RANK = 16
GLA_TAU = 16.0
GLA_HK = GLA_HEADS * GLA_DK
GLA_HV = GLA_HEADS * GLA_DV
GLA_IN = 2 * GLA_HK + 2 * GLA_HV + GLA_RANK

kernel_name = "hybrid_retnet_gla_macaron_meta"


def rmsnorm(x, g):
    xf = x.astype(jnp.float32)
    y = xf * lax.rsqrt(jnp.mean(xf * xf, axis=-1, keepdims=True) + EPS)
    return (y * g).astype(x.dtype)


def swiglu(x, w_in, w_out):
    gate, up = jnp.split(x @ w_in, 2, axis=-1)
    return (jax.nn.silu(gate) * up) @ w_out


def to_chunks(t):
    b, T, h, d = t.shape
    n = (T + PAD) // CHUNK
    t = jnp.pad(t, ((0, 0), (PAD, 0), (0, 0), (0, 0)))
    return t.reshape(b, n, CHUNK, h, d).transpose(1, 0, 3, 2, 4)


def from_chunks(t):
    n, b, h, c, d = t.shape
    return t.transpose(1, 0, 3, 2, 4).reshape(b, n * c, h, d)[:, PAD:]


def rotary(t, pos):
    half = t.shape[-1] // 2
    inv = 1.0 / (ROPE_BASE ** jnp.linspace(0.0, 1.0, half, dtype=jnp.float32))
    ang = pos.astype(jnp.float32)[:, None] * inv[None, :]
    cos = jnp.cos(ang)[None, :, None, :].astype(t.dtype)
    sin = jnp.sin(ang)[None, :, None, :].astype(t.dtype)
    t1, t2 = t[..., :half], t[..., half:]
    return jnp.concatenate([t1 * cos - t2 * sin, t1 * sin + t2 * cos], axis=-1)


def retention(h, w_in, head_norm, w_out):
    b, T, _ = h.shape
    q, k, v, g = jnp.split(h @ w_in, [D_MODEL, 2 * D_MODEL, 4 * D_MODEL], axis=-1)
    pos = jnp.arange(T)
    q = rotary(q.reshape(b, T, RET_HEADS, RET_DK), pos)
    k = rotary(k.reshape(b, T, RET_HEADS, RET_DK), pos) * (RET_DK ** -0.5)
    v = v.reshape(b, T, RET_HEADS, RET_DV)

    log_gamma = jnp.log1p(-2.0 ** (-5.0 - jnp.arange(RET_HEADS, dtype=jnp.float32)))
    idx = jnp.arange(CHUNK, dtype=jnp.float32)
    rel = idx[:, None] - idx[None, :]
    decay_intra = jnp.where(rel >= 0, jnp.exp(log_gamma[:, None, None] * jnp.maximum(rel, 0.0)), 0.0)
    decay_q = jnp.exp(log_gamma[:, None] * (idx + 1.0))[..., None]
    decay_k = jnp.exp(log_gamma[:, None] * (CHUNK - 1.0 - idx))[..., None]
    decay_chunk = jnp.exp(log_gamma * CHUNK)[:, None, None]

    def step(S, inp):
        qi, ki, vi = inp
        scores = jnp.einsum('bhid,bhjd->bhij', qi, ki) * decay_intra
        o = (jnp.einsum('bhij,bhjv->bhiv', scores, vi)
             + jnp.einsum('bhid,bhdv->bhiv', qi * decay_q, S))
        S = S * decay_chunk + jnp.einsum('bhjd,bhjv->bhdv', ki * decay_k, vi)
        return S, o

    S0 = jnp.zeros((b, RET_HEADS, RET_DK, RET_DV), jnp.float32)
    _, o = lax.scan(step, S0, (to_chunks(q), to_chunks(k), to_chunks(v)))
    o = rmsnorm(from_chunks(o), head_norm)
    o = o.reshape(b, T, RET_HEADS * RET_DV) * jax.nn.silu(g)
    return o @ w_out


def gla(h, w_in, w_gate, b_gate, head_norm, w_out):
    b, T, _ = h.shape
    q, k, v, g, z = jnp.split(h @ w_in, [GLA_HK, 2 * GLA_HK, 2 * GLA_HK + GLA_HV, 2 * GLA_HK + 2 * GLA_HV], axis=-1)
    q = q.reshape(b, T, GLA_HEADS, GLA_DK) * (GLA_DK ** -0.5)
    k = k.reshape(b, T, GLA_HEADS, GLA_DK)
    v = v.reshape(b, T, GLA_HEADS, GLA_DV)
    log_a = jax.nn.log_sigmoid((z @ w_gate + b_gate).astype(jnp.float32)) / GLA_TAU
    log_a = log_a.reshape(b, T, GLA_HEADS, GLA_DK)
    causal = jnp.tril(jnp.ones((CHUNK, CHUNK), dtype=bool))[:, :, None]

    def step(S, inp):
        qi, ki, vi, ai = inp
        bcum = jnp.cumsum(ai, axis=2)
        diff = bcum[:, :, :, None, :] - bcum[:, :, None, :, :]
        dec = jnp.exp(jnp.where(causal, diff, -jnp.inf))
        scores = jnp.einsum('bhid,bhijd,bhjd->bhij', qi, dec, ki)
        o = (jnp.einsum('bhij,bhjv->bhiv', scores, vi)
             + jnp.einsum('bhid,bhdv->bhiv', qi * jnp.exp(bcum), S))
        btot = bcum[:, :, -1:, :]
        S = (S * jnp.exp(btot)[:, :, 0, :, None]
             + jnp.einsum('bhjd,bhjv->bhdv', ki * jnp.exp(btot - bcum), vi))
        return S, o

    S0 = jnp.zeros((b, GLA_HEADS, GLA_DK, GLA_DV), jnp.float32)
    _, o = lax.scan(step, S0, (to_chunks(q), to_chunks(k), to_chunks(v), to_chunks(log_a)))
    o = rmsnorm(from_chunks(o), head_norm)
    o = o.reshape(b, T, GLA_HV) * jax.nn.silu(g)
    return o @ w_out


def setup_inputs(seed: int = 0) -> dict:
    key = jax.random.key(seed)
    ks = jax.random.split(key, 20)
    nrm = lambda k, shape, fan_in: jax.random.normal(k, shape, jnp.float32) * (fan_in ** -0.5)
    gain = lambda k, shape: 1.0 + 0.01 * jax.random.normal(k, shape, jnp.float32)
    return {
        "x": jax.random.normal(ks[0], (BATCH, SEQ, D_MODEL), jnp.float32),
        "meta_tokens": jax.random.normal(ks[1], (N_META, D_MODEL), jnp.float32),
        "norm_ffn1": gain(ks[2], (DEPTH, D_MODEL)),
        "ffn1_w_in": nrm(ks[3], (DEPTH, D_MODEL, 2 * D_FF), D_MODEL),
        "ffn1_w_out": nrm(ks[4], (DEPTH, D_FF, D_MODEL), D_FF),
        "norm_mix": gain(ks[5], (DEPTH, D_MODEL)),
        "norm_ffn2": gain(ks[6], (DEPTH, D_MODEL)),
        "ffn2_w_in": nrm(ks[7], (DEPTH, D_MODEL, 2 * D_FF), D_MODEL),
        "ffn2_w_out": nrm(ks[8], (DEPTH, D_FF, D_MODEL), D_FF),
        "ret_w_in": nrm(ks[9], (N_RET, D_MODEL, RET_IN), D_MODEL),
        "ret_head_norm": gain(ks[10], (N_RET, RET_HEADS, RET_DV)),
        "ret_w_out": nrm(ks[11], (N_RET, RET_HEADS * RET_DV, D_MODEL), RET_HEADS * RET_DV),
        "gla_w_in": nrm(ks[12], (N_GLA, D_MODEL, GLA_IN), D_MODEL),
        "gla_w_gate": nrm(ks[13], (N_GLA, GLA_RANK, GLA_HK), GLA_RANK),
        "gla_b_gate": 0.1 * jax.random.normal(ks[14], (N_GLA, GLA_HK), jnp.float32),
        "gla_head_norm": gain(ks[15], (N_GLA, GLA_HEADS, GLA_DV)),
        "gla_w_out": nrm(ks[16], (N_GLA, GLA_HV, D_MODEL), GLA_HV),
        "final_norm": gain(ks[17], (D_MODEL,)),
    }


def reference(x, meta_tokens, norm_ffn1, ffn1_w_in, ffn1_w_out, norm_mix, norm_ffn2,
              ffn2_w_in, ffn2_w_out, ret_w_in, ret_head_norm, ret_w_out,
              gla_w_in, gla_w_gate, gla_b_gate, gla_head_norm, gla_w_out, final_norm):
    b = x.shape[0]
    meta = jnp.broadcast_to(meta_tokens[None].astype(x.dtype), (b, N_META, D_MODEL))
    h = jnp.concatenate([meta, x], axis=1)
    for i in range(DEPTH):
        j = i // N_MIXERS
        h = h + 0.5 * swiglu(rmsnorm(h, norm_ffn1[i]), ffn1_w_in[i], ffn1_w_out[i])
        hn = rmsnorm(h, norm_mix[i])
        if i % N_MIXERS == 0:
            mix = retention(hn, ret_w_in[j], ret_head_norm[j], ret_w_out[j])
        else:
            mix = gla(hn, gla_w_in[j], gla_w_gate[j], gla_b_gate[j], gla_head_norm[j], gla_w_out[j])
        h = h + mix
        h = h + 0.5 * swiglu(rmsnorm(h, norm_ffn2[i]), ffn2_w_in[i], ffn2_w_out[i])
    h = rmsnorm(h, final_norm)
    return h[:, N_META:]
```

```python
import os
import numpy as np
import ml_dtypes
import concourse.bass as bass
import concourse.mybir as mybir
from concourse.bass_utils import run_bass_kernel_spmd

F32 = mybir.dt.float32
BF16 = mybir.dt.bfloat16
AF = mybir.ActivationFunctionType
ALU = mybir.AluOpType

D = 1024
NMETA = 16
SEG = 2048
NT = NMETA + SEG
DFF = 2816
NFT = DFF // 128
EPS = 1e-6
GROUPS = [(0, 16)] + [(16 + 512 * i, 16 + 512 * (i + 1)) for i in range(4)]
ENGS = ("pe", "act", "dve", "pool", "sp")
SKIP = set(os.environ.get("K_SKIP", "").split(","))


class Sched:
    def __init__(self, nc):
        self.nc = nc
        self.q = {e: [] for e in ENGS}
        self.cnt = {e: 0 for e in ENGS}
        self.known = {e: {} for e in ENGS}
        self.lastw = {}
        self.readers = {}
        self.dmacnt = {}
        self.semnames = ["p_" + e for e in ENGS]
        self.pending_nobump = {e: False for e in ENGS}

    def _deps(self, eng, reads, writes):
        deps = {}

        def add(tok):
            if tok is None:
                return
            s, v = tok
            if deps.get(s, 0) < v:
                deps[s] = v
        for r in reads:
            add(self.lastw.get(r))
        for w in writes:
            add(self.lastw.get(w))
            for t in self.readers.get(w, ()):
                add(t)
        waits = []
        own = "p_" + eng
        for s, v in deps.items():
            if s == own and eng == "pe":
                continue
            if self.known[eng].get(s, 0) >= v:
                continue
            self.known[eng][s] = v
            waits.append((s, v))
        return waits

    def _record(self, tok, reads, writes):
        for r in reads:
            self.readers.setdefault(r, []).append(tok)
        for w in writes:
            self.lastw[w] = tok
            self.readers[w] = []

    def op(self, eng, fn, reads=(), writes=(), bump=True):
        waits = self._deps(eng, reads, writes)
        own = "p_" + eng
        if bump:
            self.cnt[eng] += 1
            tok = (own, self.cnt[eng])
            inc = (own, 1)
            self.pending_nobump[eng] = False
        else:
            tok = (own, self.cnt[eng] + 1)
            inc = None
            self.pending_nobump[eng] = True
        self.q[eng].append((fn, waits, inc))
        self._record(tok, reads, writes)
        return tok

    def dma(self, eng, fn, sem, reads=(), writes=()):
        if sem not in self.dmacnt:
            self.dmacnt[sem] = 0
            self.semnames.append(sem)
        waits = self._deps(eng, reads, writes)
        self.dmacnt[sem] += 16
        tok = (sem, self.dmacnt[sem])
        self.q[eng].append((fn, waits, (sem, 16)))
        self._record(tok, reads, writes)
        return tok

    def dma_group_end(self, sem, resources):
        tok = (sem, self.dmacnt[sem])
        for r in resources:
            self.lastw[r] = tok

    def barrier(self):
        toks = [("p_" + e, self.cnt[e]) for e in ENGS if self.cnt[e] > 0]
        toks += [(s, v) for s, v in self.dmacnt.items() if v > 0]
        for e in ENGS:
            waits = []
            for s, v in toks:
                if s == "p_" + e and e == "pe":
                    continue
                if self.known[e].get(s, 0) >= v:
                    continue
                self.known[e][s] = v
                waits.append((s, v))
            if waits:
                self.q[e].append((None, waits, None))

    def final_wait(self, eng, toks):
        waits = []
        for s, v in toks:
            if self.known[eng].get(s, 0) >= v:
                continue
            self.known[eng][s] = v
            waits.append((s, v))
        self.q[eng].append((None, waits, None))

    def simulate(self):
        sem = {n: 0 for n in self.semnames}
        pc = {e: 0 for e in ENGS}
        progress = True
        while progress:
            progress = False
            for e in ENGS:
                while pc[e] < len(self.q[e]):
                    fn, waits, inc = self.q[e][pc[e]]
                    if any(sem[s] < v for s, v in waits):
                        break
                    if inc is not None:
                        sem[inc[0]] += inc[1]
                    pc[e] += 1
                    progress = True
        stuck = {e: (pc[e], len(self.q[e]), self.q[e][pc[e]][1]) for e in ENGS if pc[e] < len(self.q[e])}
        if stuck:
            raise RuntimeError("deadlock in schedule: %r ; sems=%r" % (stuck, sem))
        return {e: len(self.q[e]) for e in ENGS}

    def emit(self):
        nc = self.nc
        for e in ENGS:
            assert not self.pending_nobump[e], e
        print("sched sizes", self.simulate(), "nsems", len(self.semnames))
        from contextlib import ExitStack
        with ExitStack() as st:
            sems = {n: st.enter_context(nc.semaphore(n)) for n in self.semnames}
            block = st.enter_context(nc.Block())
            handles = {"pe": block.tensor, "act": block.scalar, "dve": block.vector,
                       "pool": block.gpsimd, "sp": block.sync}

            def mk(ename):
                def body(e):
                    for fn, waits, inc in self.q[ename]:
                        for s, v in waits:
                            e.wait_ge(sems[s], v)
                        if fn is None:
                            continue
                        ins = fn(e)
                        if inc is not None:
                            ins.then_inc(sems[inc[0]], inc[1])
                return body
            for ename in ENGS:
                handles[ename](mk(ename))


class Arena:
    def __init__(self, big, nbytes):
        self.big = big
        self.nbytes = nbytes
        self.off = 0
        self.marks = []

    def alloc(self, nbytes, dtype, shape=None):
        rb = (nbytes + 63) // 64 * 64
        assert self.off + rb <= self.nbytes, (self.off, rb, self.nbytes)
        a = self.big[:, self.off // 2:(self.off + nbytes) // 2]
        self.off += rb
        if dtype is not BF16:
            a = a.bitcast(dtype)
        return a

    def mark(self):
        self.marks.append(self.off)

    def release(self):
        self.off = self.marks.pop()


def v3(ap, a):
    return ap.rearrange("p (a b) -> p a b", a=a)


class Builder:
    def __init__(self, stage):
        self.stage = stage
        self.nc = bass.Bass("TRN2", target_bir_lowering=False)
        self.S = Sched(self.nc)
        self.inputs = {}
        self.outputs = {}
        self.psum_rr = 0
        self.uid = 0
        self.dbgtoks = []

    def din(self, name, shape, dtype=F32):
        t = self.nc.dram_tensor(name, list(shape), dtype, kind="ExternalInput").ap()
        self.inputs[name] = t
        return t

    def dout(self, name, shape, dtype=F32):
        t = self.nc.dram_tensor(name, list(shape), dtype, kind="ExternalOutput").ap()
        self.outputs[name] = t
        return t

    def newid(self, p="r"):
        self.uid += 1
        return "%s%d" % (p, self.uid)

    def dbg(self, name, ap, shape, dtype, reads):
        if "dump" not in SKIP:
            return
        o = self.dout("dbg_" + name, shape, dtype)
        self.dbgtoks.append(self.S.dma("sp", lambda e: e.dma_start(out=o, in_=ap), "dbg_" + name, reads, ["dbg_" + name]))

    def ps(self, dtype=F32):
        b = self.psum_rr
        self.psum_rr = (self.psum_rr + 1) % 8
        ap = self.psum[:, b * 512:(b + 1) * 512]
        if dtype is BF16:
            ap = ap.bitcast(BF16)
        return ap, ("ps", b)

    def mm(self, out, lhsT, rhs, start, stop, reads, writes, bump=None):
        if bump is None:
            bump = stop
        self.S.op("pe", lambda e: e.matmul(out, lhsT, rhs, start=start, stop=stop), reads, writes, bump=bump)

    def tr(self, out, in_, ident, reads, writes, bump=True):
        self.S.op("pe", lambda e: e.transpose(out, in_, ident), reads, writes, bump=bump)

    def act(self, out, in_, func, reads, writes, bias=None, scale=None, accum_out=None):
        kw = {}
        if bias is not None:
            kw["bias"] = bias
        if scale is not None:
            kw["scale"] = scale
        if accum_out is not None:
            kw["accum_out"] = accum_out
        self.S.op("act", lambda e: e.activation(out, in_, func, **kw), reads, writes)

    def stt(self, eng, out, in0, scalar, in1, op0, op1, reads, writes):
        self.S.op(eng, lambda e: e.scalar_tensor_tensor(out, in0, scalar, in1, op0, op1), reads, writes)

    def tt(self, eng, out, in0, in1, op, reads, writes):
        self.S.op(eng, lambda e: e.tensor_tensor(out, in0, in1, op), reads, writes)

    def ts(self, eng, out, in0, s1, s2, op0, op1, reads, writes):
        if op1 is None:
            self.S.op(eng, lambda e: e.tensor_scalar(out, in0, s1, None, op0), reads, writes)
        else:
            self.S.op(eng, lambda e: e.tensor_scalar(out, in0, s1, s2, op0, op1), reads, writes)

    def cp(self, eng, out, in_, reads, writes):
        if eng == "act":
            self.S.op("act", lambda e: e.activation(out, in_, AF.Copy), reads, writes)
        else:
            self.S.op(eng, lambda e: e.tensor_copy(out, in_), reads, writes)

    def wload(self, dst, src, sem, reads, writes, chunk=4096):
        self.S.dma("pool", lambda e: e.dma_start(out=dst, in_=src, max_dma_last_dim=chunk), sem, reads, writes)

    def load(self, dst, src, sem, reads, writes, eng="sp"):
        self.S.dma(eng, lambda e: e.dma_start(out=dst, in_=src), sem, reads, writes)

    def build(self):
        nc = self.nc
        from contextlib import ExitStack
        with ExitStack() as st:
            SB_BYTES = 204 * 1024
            big = st.enter_context(nc.sbuf_tensor("big", [128, SB_BYTES // 2], BF16))
            self.psum = st.enter_context(nc.psum_tensor("psum", [128, 8 * 512], F32))
            self.A = Arena(big, SB_BYTES)
            self.program()
            self.S.emit()
        return nc

    def consts(self):
        A = self.A
        self.ident = A.alloc(128 * 2, BF16)
        self.ones = A.alloc(128 * 2, BF16)
        self.maskT = A.alloc(128 * 4, F32)
        self.gcols = A.alloc(7 * 8 * 4, F32)
        self.epsc = A.alloc(64, F32)
        d_ident = self.din("ident", [128, 128], BF16)
        d_ones = self.din("ones", [128, 128], BF16)
        d_mask = self.din("maskT", [128, 128])
        d_g = self.din("gcols", [128, 56])
        d_eps = self.din("epsc", [128, 16])
        self.load(self.ident, d_ident, "c_ld", [], ["ident"])
        self.load(self.ones, d_ones, "c_ld", [], ["ones"])
        self.load(self.maskT, d_mask, "c_ld", [], ["maskT"])
        self.load(self.gcols, d_g, "c_ld", [], ["gcols"])
        self.load(self.epsc, d_eps, "c_ld", [], ["epsc"])
        self.S.dma_group_end("c_ld", ["ident", "ones", "maskT", "gcols", "epsc"])

    def gcol(self, which, k):
        i = which * 8 + k
        return self.gcols[:, i:i + 1]

    def norm_group(self, g, which, xn_view, xn_res, sq, rstd, sqres=("sq",)):
        t0, t1 = GROUPS[g]
        n = t1 - t0
        sqres = list(sqres)
        hres = [("h", k, g) for k in range(8)]
        sqv = v3(sq, 8)[:, :, :n]
        self.act(sqv, self.hT[:, :, t0:t1], AF.Square, hres, sqres)
        ps, pr = self.ps()
        for k in range(8):
            self.mm(ps[:, :n], self.ones, v3(sq, 8)[:, k, :n], k == 0, k == 7, sqres + ["ones"], [pr])
        self.act(rstd[:, :n], ps[:, :n], AF.Sqrt, [pr, "epsc"], ["rstd"], bias=self.epsc[:, 0:1], scale=1.0 / D)
        self.S.op("dve", lambda e: e.reciprocal(rstd[:, :n], rstd[:, :n]), ["rstd"], ["rstd"])
        for k in range(8):
            eng = "dve"
            self.stt(eng, xn_view[:, k, :n], self.hT[:, k, t0:t1], self.gcol(which, k), rstd[:, :n],
                     ALU.mult, ALU.mult, [("h", k, g), "rstd", "gcols"], [(xn_res, k)])

    def ffn(self, idx, which_norm, w1_d, w2_d):
        A = self.A
        A.mark()
        xn = A.alloc(8 * NT * 2, BF16)
        xn3 = v3(xn, 8)
        actb = A.alloc(11 * NT * 2, BF16)
        act3 = v3(actb, 11)
        w1s = [A.alloc(8 * 256 * 2, BF16) for _ in range(3)]
        w2 = A.alloc(11 * 1024 * 2, BF16)
        w23 = v3(w2, 11)
        rstd = A.alloc(512 * 4, F32)
        stmp = [A.alloc(512 * 4, F32) for _ in range(2)]
        sq = A.alloc(8 * 512 * 2, BF16)
        tag = "f%d" % idx

        def load_w1(j):
            slot = j % 3
            self.wload(w1s[slot], w1_d[j].rearrange("p k c -> p (k c)"), "w1s%d" % slot, [], [("w1", slot)])

        load_w1(0)
        load_w1(1)
        for jj in range(11):
            self.wload(w23[:, jj, :], w2_d[jj], "w2s", [], [("w2", jj)])
        self.S.dma_group_end("w2s", [("w2", jj) for jj in range(11)])
        for g in range(5):
            t0, t1 = GROUPS[g]
            if "norm" in SKIP:
                continue
            self.norm_group(g, which_norm, xn3[:, :, t0:t1], (tag + "xn", g), sq, rstd)
        cnt = 0
        for half in range(2):
            if half == 1:
                for jj in range(11):
                    self.wload(w23[:, jj, :], w2_d[11 + jj], "w2s", [], [("w2", jj)])
                self.S.dma_group_end("w2s", [("w2", jj) for jj in range(11)])
            for jj in range(11):
                j = half * 11 + jj
                slot = j % 3
                w13 = v3(w1s[slot], 8)
                for g in range(5):
                    if "p1" in SKIP:
                        continue
                    t0, t1 = GROUPS[g]
                    n = t1 - t0
                    pg, rg = self.ps()
                    pu, ru = self.ps()
                    xr = [((tag + "xn", g), k) for k in range(8)]
                    for k in range(8):
                        self.mm(pg[:, :n], w13[:, k, 0:128], xn3[:, k, t0:t1], k == 0, k == 7,
                                [("w1", slot), xr[k]], [rg])
                    for k in range(8):
                        self.mm(pu[:, :n], w13[:, k, 128:256], xn3[:, k, t0:t1], k == 0, k == 7,
                                [("w1", slot), xr[k]], [ru])
                    tmp = stmp[cnt % 2]
                    tr_ = ("stmp", cnt % 2)
                    cnt += 1
                    self.act(tmp[:, :n], pg[:, :n], AF.Silu, [rg], [tr_])
                    self.tt("dve", act3[:, jj, t0:t1], pu[:, :n], tmp[:, :n], ALU.mult, [ru, tr_], [("act", jj, g)])
                if j + 2 < NFT:
                    load_w1(j + 2)
            for g in range(5):
                if "p2" in SKIP:
                    continue
                t0, t1 = GROUPS[g]
                n = t1 - t0
                for dt in range(8):
                    po, ro = self.ps()
                    for jj in range(11):
                        self.mm(po[:, :n], w23[:, jj, dt * 128:(dt + 1) * 128], act3[:, jj, t0:t1], jj == 0, jj == 10,
                                [("w2", jj), ("act", jj, g)], [ro])
                    self.stt("dve", self.hT[:, dt, t0:t1], po[:, :n], 0.5, self.hT[:, dt, t0:t1], ALU.mult, ALU.add,
                             [ro, ("h", dt, g)], [("h", dt, g)])
        self.S.barrier()
        A.release()


    def mixer(self, kind, passB, which_norm, d):
        A = self.A
        A.mark()
        ret = kind == "ret"
        H = 4
        ndt = 2 if ret else 1
        dk = 128 * ndt
        dv = 512 if ret else 256
        nqt = H * ndt
        nvt = dv // 128
        nft = H * nvt
        tg = kind + ("B" if passB else "A")
        gam = [1.0 - 2.0 ** (-5.0 - h) for h in range(H)]
        hn = A.alloc(8 * 512 * 2, BF16); hn3 = v3(hn, 8)
        qT = A.alloc(nqt * 512 * 2, BF16); qT3 = v3(qT, nqt)
        kT = A.alloc(nqt * 512 * 2, BF16); kT3 = v3(kT, nqt)
        kTM = A.alloc(4 * H * dk * 2, BF16); kTM3 = v3(kTM, 4)
        vTM = A.alloc(4 * dv * 2, BF16); vTM3 = v3(vTM, 4)
        gw = A.alloc(4 * dv * 2, BF16); gw3 = v3(gw, 4)
        yTM = A.alloc(4 * dv * 2, BF16); yTM3 = v3(yTM, 4)
        scr = A.alloc(16 * 512 * 2, BF16)
        yT3 = v3(scr[:, :nft * 512], nft)
        sq = scr[:, :8 * 512]
        Ltmp = scr.bitcast(F32)[:, :nqt * dv]
        Ltmp3 = v3(Ltmp, nqt)
        X = A.alloc(nqt * dv * 4, F32); X3 = v3(X, nqt)
        Sbf = A.alloc(nqt * dv * 2, BF16); Sbf3 = v3(Sbf, nqt)
        wm = [A.alloc(8 * 512 * 2, BF16) for _ in range(2)]
        wo = [A.alloc(nft * 128 * 2, BF16) for _ in range(1)]
        rstd = A.alloc(512 * 4, F32)
        tmps = [A.alloc(512 * 4, F32) for _ in range(4)]
        gtmp = tmps[2]
        junk = tmps[3]
        scr_res = [(tg + "yT", h_) for h_ in range(H)]
        coef_sb = A.alloc(12 * 4, F32)
        PTb = A.alloc(128 * 2, BF16)
        small = A.alloc(64 * 4, F32)
        ssq = small[:, 0:1]
        rso = small[:, 1:2]
        hnB = A.alloc(H * dv * 4, F32)
        self.load(hnB, d["hnormB"], tg + "_c", [], [tg + "hnB"])
        if ret:
            tabs = A.alloc(4 * 512 * 4, F32); tabs3 = v3(tabs, 4)
        else:
            wz = A.alloc(8 * 16 * 2, BF16); wz3 = v3(wz, 8)
            wgt = A.alloc(512 * 2, BF16)
            gcst = A.alloc(16 * 4, F32)
            rmask = A.alloc(512 * 4, F32)
            zTb = A.alloc(512 * 2, BF16)
            bcum = A.alloc(H * 512 * 4, F32); bcum3 = v3(bcum, H)
            Eq = A.alloc(H * 512 * 4, F32); Eq3 = v3(Eq, H)
            Ek = A.alloc(H * 512 * 4, F32); Ek3 = v3(Ek, H)
            dcols = A.alloc(H * 20 * 4, F32); dcols3 = v3(dcols, H)
            Bacc = A.alloc(4 * 4, F32)
            self.wload(wz, d["wz"].rearrange("p k c -> p (k c)"), tg + "_wz", [], [tg + "wz"])
            self.wload(wgt[0:16, :], d["wgate"], tg + "_wg", [], [tg + "wgt"])
            self.load(gcst, d["gcst"], tg + "_c2", [], [tg + "gcst"])
            self.ts("dve", gcst[:, 0:4], gcst[:, 0:4], -1.0, None, ALU.mult, None, [tg + "gcst"], [tg + "gcst"])
            self.load(rmask, d["rmask"], tg + "_c3", [], [tg + "rmask"])
            self.S.op("dve", lambda e: e.memset(Bacc, 0.0), [], [tg + "Bacc"])
        self.S.op("dve", lambda e: e.memset(X, 0.0), [], [(tg + "X", j) for j in range(nqt)])
        self.S.op("pool", lambda e: e.memset(Sbf, 0.0), [], [(tg + "S", j) for j in range(nqt)])
        if passB:
            if ret:
                self.load(coef_sb, d["coef"], tg + "_cf", [], [tg + "coef"])
            else:
                Bsb = A.alloc(12 * 4, F32)
                sf = A.alloc(12 * 4, F32)
                cum = A.alloc(12 * 4, F32)
                self.load(v3(Bsb, 3), d["Bg"].rearrange("r p h -> p r h"), tg + "_cf", [], [tg + "Bsb"])
                self.load(sf, d["selflag"], tg + "_cf2", [], [tg + "sf"])
                for r in range(3):
                    cr = cum[:, r * 4:(r + 1) * 4]
                    self.ts("dve", cr, Bsb[:, 0:4], sf[:, r * 3:r * 3 + 1], None, ALU.mult, None, [tg + "Bsb", tg + "sf"], [tg + "cum"])
                    for r2 in (1, 2):
                        self.stt("dve", cr, Bsb[:, r2 * 4:(r2 + 1) * 4], sf[:, r * 3 + r2:r * 3 + r2 + 1], cr, ALU.mult, ALU.add,
                                 [tg + "Bsb", tg + "sf", tg + "cum"], [tg + "cum"])
                self.act(coef_sb, cum, AF.Exp, [tg + "cum"], [tg + "coef"])
                for r in range(3):
                    self.ts("dve", coef_sb[:, r * 4:(r + 1) * 4], coef_sb[:, r * 4:(r + 1) * 4], sf[:, 9 + r:10 + r], None,
                            ALU.mult, None, [tg + "coef", tg + "sf"], [tg + "coef"])
        dX = [1.0] * H
        win = d["win"]
        wslot = [None, None]
        wcnt = [0]

        def wslice(i):
            for s in range(2):
                if wslot[s] == i:
                    return v3(wm[s], 8), (tg + "wm", s)
            s = wcnt[0] % 2
            wcnt[0] += 1
            wslot[s] = i
            self.wload(wm[s], win[i].rearrange("p k c -> p (k c)"), tg + "_wm%d" % s, [], [(tg + "wm", s)])
            return v3(wm[s], 8), (tg + "wm", s)

        chunk_global = 0
        for g in range(5):
            t0, t1 = GROUPS[g]
            n = t1 - t0
            C = 16 if g == 0 else 128
            nch = n // C
            self.norm_group(g, which_norm, hn3[:, :, :n], tg + "hn", sq[:, :], rstd, sqres=scr_res)
            hres = [(tg + "hn", k) for k in range(8)]
            if not ret:
                pz, rz = self.ps()
                for k in range(8):
                    self.mm(pz[:16, :n], wz3[:, k, :], hn3[:, k, :n], k == 0, k == 7, [tg + "wz", hres[k]], [rz])
                self.cp("act", zTb[0:16, :n], pz[:16, :n], [rz], [tg + "zT"])
                for h in range(H):
                    pl, rl = self.ps()
                    self.mm(pl[:, :n], wgt[0:16, h * 128:(h + 1) * 128], zTb[0:16, :n], True, True, [tg + "wgt", tg + "zT"], [rl])
                    self.act(tmps[0][:, :n], pl[:, :n], AF.Exp, [rl, tg + "gcst"], ["tmp0"], bias=gcst[:, h:h + 1], scale=-1.0)
                    self.act(tmps[1][:, :n], tmps[0][:, :n], AF.Ln, ["tmp0", tg + "gcst"], ["tmp1"], bias=gcst[:, 6:7], scale=1.0)
                    rm = rmask[:, 1:17] if g == 0 else rmask[:, :n]
                    self.S.op("dve", (lambda o=bcum3[:, h, :n], a=rm, b=tmps[1][:, :n]:
                                      (lambda e: e.tensor_tensor_scan(o, a, b, 0.0, ALU.mult, ALU.add)))(),
                              ["tmp1", tg + "rmask"], [(tg + "bc", h)])
                    self.act(Eq3[:, h, :n], bcum3[:, h, :n], AF.Exp, [(tg + "bc", h), tg + "gcst"], [(tg + "Eq", h)],
                             bias=gcst[:, 5:6], scale=-1.0 / 16.0)
                    self.act(Ek3[:, h, :n], bcum3[:, h, :n], AF.Exp, [(tg + "bc", h)], [(tg + "Ek", h)], scale=1.0 / 16.0)
                    if g == 0:
                        self.ts("dve", Ek3[:, h, :n], Ek3[:, h, :n], gcst[:, 4:5], None, ALU.mult, None,
                                [(tg + "Ek", h), tg + "gcst"], [(tg + "Ek", h)])
                    lastcols = bcum3[:, h, C - 1:n:C]
                    self.act(dcols3[:, h, chunk_global:chunk_global + nch], lastcols, AF.Exp, [(tg + "bc", h)],
                             [(tg + "dc", h)], scale=-1.0 / 16.0)
                    if (not passB) and g > 0:
                        for c in range(nch):
                            col = bcum3[:, h, c * C + C - 1:c * C + C]
                            self.tt("dve", Bacc[:, h:h + 1], Bacc[:, h:h + 1], col, ALU.add, [(tg + "bc", h), tg + "Bacc"], [tg + "Bacc"])
            kinds = ("q", "k") if passB else ("k",)
            for h in range(H):
                for qk in kinds:
                    dstT = qT3 if qk == "q" else kT3
                    base = 0 if qk == "q" else nqt
                    pss = []
                    for dt in range(ndt):
                        j = base + h * ndt + dt
                        wv_, wr = wslice(j // 4)
                        pp, pr = self.ps()
                        for k in range(8):
                            self.mm(pp[:, :n], wv_[:, k, (j % 4) * 128:(j % 4 + 1) * 128], hn3[:, k, :n], k == 0, k == 7,
                                    [wr, hres[k]], [pr])
                        pss.append((pp, pr))
                    dres = [(tg + qk + "T", h * ndt + dt) for dt in range(ndt)]
                    if ret:
                        if qk == kinds[0]:
                            self.S.dma("sp", (lambda h=h, t0=t0, t1=t1, n=n: (lambda e: e.dma_start(
                                out=tabs3[:, :, :n], in_=d["rot"][h, :, :, t0:t1].rearrange("a p t -> p a t"))))(),
                                tg + "_tab", [], [tg + "tabs"])
                        ci, si = (0, 1) if qk == "q" else (2, 3)
                        (pa, ra), (pb, rb) = pss
                        cT = tabs3[:, ci, :n]
                        sT = tabs3[:, si, :n]
                        tb = tg + "tabs"
                        self.tt("dve", tmps[0][:, :n], pa[:, :n], cT, ALU.mult, [ra, tb], ["tmp0"])
                        self.tt("dve", tmps[1][:, :n], pb[:, :n], sT, ALU.mult, [rb, tb], ["tmp1"])
                        self.tt("pool", dstT[:, h * 2, :n], tmps[0][:, :n], tmps[1][:, :n], ALU.subtract, ["tmp0", "tmp1"], [dres[0]])
                        self.tt("dve", tmps[2][:, :n], pa[:, :n], sT, ALU.mult, [ra, tb], ["tmp2"])
                        self.tt("dve", tmps[3][:, :n], pb[:, :n], cT, ALU.mult, [rb, tb], ["tmp3"])
                        self.tt("pool", dstT[:, h * 2 + 1, :n], tmps[2][:, :n], tmps[3][:, :n], ALU.add, ["tmp2", "tmp3"], [dres[1]])
                    else:
                        E3 = Eq3 if qk == "q" else Ek3
                        er = (tg + ("Eq" if qk == "q" else "Ek"), h)
                        pp, pr = pss[0]
                        self.tt("dve", dstT[:, h, :n], pp[:, :n], E3[:, h, :n], ALU.mult, [pr, er], [dres[0]])
            if passB and (not ret) and g == 1:
                self.dbg("qT", qT, [128, nqt * 512], BF16, [(tg + "qT", j) for j in range(nqt)])
                self.dbg("kT", kT, [128, nqt * 512], BF16, [(tg + "kT", j) for j in range(nqt)])
                self.dbg("Eq", Eq, [128, H * 512], F32, [(tg + "Eq", j) for j in range(H)])
                self.dbg("Ek", Ek, [128, H * 512], F32, [(tg + "Ek", j) for j in range(H)])
                self.dbg("coef", coef_sb, [128, 12], F32, [tg + "coef"])
            for c in range(nch):
                for j0 in range(0, nqt, 4):
                    pt, prt = self.ps(BF16)
                    for jj in range(4):
                        j = j0 + jj
                        self.tr(pt[:C, jj * 128:(jj + 1) * 128], kT3[:, j, c * C:(c + 1) * C], self.ident,
                                [(tg + "kT", j), "ident"], [prt], bump=(jj == 3))
                    self.cp("act", kTM3[:C, c, j0 * 128:(j0 + 4) * 128], pt[:C, :512], [prt], [(tg + "kTM", c)])
            if passB and g == 1:
              for h in range(H):
                  for dt in range(ndt):
                      j = h * ndt + dt
                      self.ts("dve", X3[:, j, :], X3[:, j, :], dX[h], None, ALU.mult, None, [(tg + "X", j)], [(tg + "X", j)])
                  for r in range(3):
                      self.load(Ltmp3[:, h * ndt:(h + 1) * ndt, :], d["Lg"][r][:, h * ndt:(h + 1) * ndt, :], tg + "_lt",
                                [], scr_res)
                      for dt in range(ndt):
                          j = h * ndt + dt
                          cf = coef_sb[:, r * 4 + h:r * 4 + h + 1]
                          self.stt("dve", X3[:, j, :], Ltmp3[:, j, :], cf, X3[:, j, :], ALU.mult, ALU.add,
                                   scr_res + [(tg + "X", j), tg + "coef"], [(tg + "X", j)])
                  for dt in range(ndt):
                      j = h * ndt + dt
                      self.cp("act", Sbf3[:, j, :], X3[:, j, :], [(tg + "X", j)], [(tg + "S", j)])
                  dX[h] = 1.0
            for h in range(H):
                if ret:
                    vsl, voff = 4 + h, 0
                    gsl, goff = 8 + h, 0
                else:
                    vsl, voff = 2 + h // 2, (h % 2) * 256
                    gsl, goff = 4 + h // 2, (h % 2) * 256
                wv_, wr = wslice(vsl)
                for c in range(nch):
                    pv, prv = self.ps()
                    for k in range(8):
                        self.mm(pv[:C, :dv], hn3[:, k, c * C:(c + 1) * C], wv_[:, k, voff:voff + dv], k == 0, k == 7,
                                [wr, hres[k]], [prv])
                    self.cp("act", vTM3[:C, c, :], pv[:C, :dv], [prv], [(tg + "v", c)])
                if passB:
                    wg_, wgr = wslice(gsl)
                    for c in range(nch):
                        pg_, prg = self.ps()
                        for k in range(8):
                            self.mm(pg_[:C, :dv], hn3[:, k, c * C:(c + 1) * C], wg_[:, k, goff:goff + dv], k == 0, k == 7,
                                    [wgr, hres[k]], [prg])
                        self.act(gtmp[:C, :dv], pg_[:C, :dv], AF.Silu, [prg], ["tmp2"])
                        self.tt("pool", gw3[:C, c, :], gtmp[:C, :dv], hnB[:C, h * dv:(h + 1) * dv], ALU.mult,
                                ["tmp2", tg + "hnB"], [(tg + "gw", c)])
                for c in range(nch):
                    cg = chunk_global + c
                    c0, c1 = c * C, (c + 1) * C
                    last_chunk = (g == 4 and c == nch - 1)
                    if passB:
                        psc, rsc = self.ps()
                        for dt in range(ndt):
                            j = h * ndt + dt
                            self.mm(psc[:C, :C], kT3[:, j, c0:c1], qT3[:, j, c0:c1], dt == 0, dt == ndt - 1,
                                    [(tg + "kT", j), (tg + "qT", j)], [rsc])
                        self.tt("dve", PTb[:C, :C], psc[:C, :C], self.maskT[:C, :C], ALU.mult, [rsc, "maskT"], [tg + "PT"])
                        po, ro = self.ps()
                        self.mm(po[:C, :dv], PTb[:C, :C], vTM3[:C, c, :], True, False, [tg + "PT", (tg + "v", c)], [ro], bump=False)
                        for dt in range(ndt):
                            j = h * ndt + dt
                            self.mm(po[:C, :dv], qT3[:, j, c0:c1], Sbf3[:, j, :], False, dt == ndt - 1,
                                    [(tg + "qT", j), (tg + "S", j)], [ro])
                    if not (passB and last_chunk):
                        dc = (gam[h] ** C) if ret else dcols3[:, h, cg:cg + 1]
                        dcr = [] if ret else [(tg + "dc", h)]
                        for dt in range(ndt):
                            j = h * ndt + dt
                            pu, ru = self.ps()
                            self.mm(pu[:, :dv], kTM3[:C, c, j * 128:(j + 1) * 128], vTM3[:C, c, :], True, True,
                                    [(tg + "kTM", c), (tg + "v", c)], [ru])
                            self.stt("dve", X3[:, j, :], X3[:, j, :], dX[h], pu[:, :dv], ALU.mult, ALU.add,
                                     [(tg + "X", j), ru] + dcr, [(tg + "X", j)])
                            if passB:
                                self.ts("pool", Sbf3[:, j, :], X3[:, j, :], dc, None, ALU.mult, None,
                                        [(tg + "X", j)] + dcr, [(tg + "S", j)])
                        dX[h] = dc
                    if passB:
                        self.act(junk[:C, :dv], po[:C, :dv], AF.Square, [ro], ["tmp3", tg + "ssq"], accum_out=ssq[:C, :])
                        self.act(rso[:C, :], ssq[:C, :], AF.Sqrt, [tg + "ssq", "epsc"], [tg + "rso"], bias=self.epsc[:C, 0:1], scale=1.0 / dv)
                        self.S.op("dve", (lambda a=rso[:C, :]: (lambda e: e.reciprocal(a, a)))(), [tg + "rso"], [tg + "rso"])
                        self.stt("dve", yTM3[:C, c, :], po[:C, :dv], rso[:C, :], gw3[:C, c, :], ALU.mult, ALU.mult,
                                 [ro, tg + "rso", (tg + "gw", c)], [(tg + "y", c)])
                if passB:
                    for c in range(nch):
                        pt, prt = self.ps(BF16)
                        for vt in range(nvt):
                            self.tr(pt[:, vt * C:(vt + 1) * C], yTM3[:C, c, vt * 128:(vt + 1) * 128], self.ident[:C, :C],
                                    [(tg + "y", c), "ident"], [prt], bump=(vt == nvt - 1))
                        self.cp("act", yT3[:, h * nvt:(h + 1) * nvt, c * C:(c + 1) * C],
                                pt[:, :nvt * C].rearrange("p (a b) -> p a b", a=nvt), [prt], [(tg + "yT", h)])
            if passB and (not ret) and g == 1:
                self.dbg("yT", scr[:, :nft * 512], [128, nft * 512], BF16, scr_res)
                self.dbg("X", X, [128, nqt * dv], F32, [(tg + "X", j) for j in range(nqt)])
                self.dbg("v3", vTM, [128, 4 * dv], BF16, [(tg + "v", c) for c in range(4)])
                self.dbg("gw3", gw, [128, 4 * dv], BF16, [(tg + "gw", c) for c in range(4)])
                self.dbg("yTM3", yTM, [128, 4 * dv], BF16, [(tg + "y", c) for c in range(4)])
            chunk_global += nch
            if passB:
                for dt in range(8):
                    s = 0
                    self.wload(wo[s], d["wout"][dt].rearrange("p k c -> p (k c)"), tg + "_wo%d" % s, [], [(tg + "wo", s)])
                    wo3 = v3(wo[s], nft)
                    pm, rm_ = self.ps()
                    for ft in range(nft):
                        self.mm(pm[:, :n], wo3[:, ft, :], yT3[:, ft, :n], ft == 0, ft == nft - 1,
                                [(tg + "wo", s), (tg + "yT", ft // nvt)], [rm_])
                    self.stt("dve", self.hT[:, dt, t0:t1], pm[:, :n], 1.0, self.hT[:, dt, t0:t1], ALU.mult, ALU.add,
                             [rm_, ("h", dt, g)], [("h", dt, g)])
        toks = []
        if not passB:
            for h in range(H):
                for dt in range(ndt):
                    j = h * ndt + dt
                    dcr = [] if ret else [(tg + "dc", h)]
                    self.ts("dve", X3[:, j, :], X3[:, j, :], dX[h], None, ALU.mult, None, [(tg + "X", j)] + dcr, [(tg + "X", j)])
            toks.append(self.S.dma("sp", lambda e: e.dma_start(out=d["Lout"], in_=X), tg + "_lo",
                                   [(tg + "X", j) for j in range(nqt)], [tg + "Lout"]))
            if not ret:
                self.ts("dve", Bacc, Bacc, -1.0 / 16.0, None, ALU.mult, None, [tg + "Bacc"], [tg + "Bacc"])
                toks.append(self.S.dma("sp", lambda e: e.dma_start(out=d["Bout"], in_=Bacc), tg + "_bo", [tg + "Bacc"], [tg + "Bout"]))
        self.S.barrier()
        A.release()
        return toks

    def final_norm(self, which, out_d):
        A = self.A
        A.mark()
        sq = A.alloc(8 * 512 * 2, BF16)
        rstd = A.alloc(512 * 4, F32)
        ob = [A.alloc(8 * 512 * 4, F32) for _ in range(2)]
        toks = []
        for g in range(1, 5):
            t0, t1 = GROUPS[g]
            o3 = v3(ob[g % 2], 8)
            self.norm_group(g, which, o3, ("fo", g % 2), sq, rstd)
            tok = self.S.dma("sp", (lambda o3=o3, t0=t0, t1=t1: (lambda e: e.dma_start(out=out_d[:, :, t0 - 16:t1 - 16], in_=o3)))(),
                             "o_st%d" % (g % 2), [(("fo", g % 2), k) for k in range(8)], [("outd", g)])
            toks.append(tok)
        self.S.final_wait("sp", toks)
        A.release()

    def load_h(self, src_d):
        for g in range(5):
            t0, t1 = GROUPS[g]
            self.load(self.hT[:, :, t0:t1], src_d[:, :, t0:t1], "h_ld%d" % g, [], [("h", k, g) for k in range(8)])

    def store_h(self, dst_d):
        toks = []
        for g in range(5):
            t0, t1 = GROUPS[g]
            tok = self.S.dma("sp", (lambda t0=t0, t1=t1: (lambda e: e.dma_start(out=dst_d[:, :, t0:t1], in_=self.hT[:, :, t0:t1])))(),
                             "h_st", [("h", k, g) for k in range(8)], [("hout", g)])
            toks.append(tok)
        return toks

    def program(self):
        A = self.A
        hT = A.alloc(8 * NT * 4, F32)
        self.hT = v3(hT, 8)
        self.consts()
        st = self.stage
        if st == "ffn_only":
            xT = self.din("xT", [128, 8, NT])
            w1 = self.din("w1_0", [NFT, 128, 8, 256])
            w2 = self.din("w2_0", [NFT, 128, 1024])
            hout = self.dout("hout", [128, 8, NT])
            self.load_h(xT)
            self.ffn(0, 0, w1, w2)
            toks = self.store_h(hout)
            self.S.final_wait("sp", toks)
            return

        def ffn_in(i):
            return self.din("w1_%d" % i, [NFT, 128, 8, 256]), self.din("w2_%d" % i, [NFT, 128, 1024])

        def ret_in(passB):
            d = {"win": self.din("ret_win", [12, 128, 8, 512]), "rot": self.din("rot", [4, 4, 128, NT]),
                 "hnormB": self.din("ret_hnB", [128, 2048])}
            if passB:
                d["wout"] = self.din("ret_wout", [8, 128, 16, 128])
                d["Lg"] = self.din("ret_Lg", [3, 128, 8, 512])
                d["coef"] = self.din("ret_coef", [128, 12])
            else:
                d["Lout"] = self.dout("ret_L", [128, 8 * 512])
            return d

        def gla_in(passB):
            d = {"win": self.din("gla_win", [6, 128, 8, 512]), "wz": self.din("gla_wz", [128, 8, 16]),
                 "wgate": self.din("gla_wgate", [16, 512]), "gcst": self.din("gla_gcst", [128, 16]),
                 "rmask": self.din("gla_rmask", [128, 512]), "hnormB": self.din("gla_hnB", [128, 1024])}
            if passB:
                d["wout"] = self.din("gla_wout", [8, 128, 8, 128])
                d["Lg"] = self.din("gla_Lg", [3, 128, 4, 256])
                d["Bg"] = self.din("gla_Bg", [3, 128, 4])
                d["selflag"] = self.din("gla_selflag", [128, 12])
            else:
                d["Lout"] = self.dout("gla_L", [128, 4 * 256])
                d["Bout"] = self.dout("gla_B", [128, 4])
            return d

        toks = []
        if st == "l1":
            self.load_h(self.din("xT", [128, 8, NT]))
            self.ffn(0, 0, *ffn_in(0))
            toks += self.mixer("ret", False, 1, ret_in(False))
            toks += self.store_h(self.dout("hout", [128, 8, NT]))
        elif st == "l2":
            self.load_h(self.din("hin", [128, 8, NT]))
            self.mixer("ret", True, 1, ret_in(True))
            self.ffn(1, 2, *ffn_in(1))
            self.ffn(2, 3, *ffn_in(2))
            toks += self.mixer("gla", False, 4, gla_in(False))
            toks += self.store_h(self.dout("hout", [128, 8, NT]))
        elif st == "l3":
            self.load_h(self.din("hin", [128, 8, NT]))
            self.mixer("gla", True, 4, gla_in(True))
            if "dbg3" in SKIP:
                toks += self.store_h(self.dout("hdbg", [128, 8, NT]))
            self.ffn(3, 5, *ffn_in(3))
            self.final_norm(6, self.dout("outT", [128, 8, SEG]))
        self.S.final_wait("sp", toks + self.dbgtoks)


def _fm(arr):
    T = arr.shape[0]
    return np.ascontiguousarray(arr.reshape(T, 8, 128).transpose(2, 1, 0))


def _w1_layout(w_in):
    g = w_in[:, :DFF].reshape(8, 128, NFT, 128)
    u = w_in[:, DFF:].reshape(8, 128, NFT, 128)
    w = np.concatenate([g, u], axis=3)
    return np.ascontiguousarray(w.transpose(2, 1, 0, 3))


def _w2_layout(w_out):
    return np.ascontiguousarray(w_out.reshape(NFT, 128, D))


def _gcols(vecs):
    out = np.zeros((128, 56), np.float32)
    for i, v in enumerate(vecs):
        out[:, i * 8:(i + 1) * 8] = v.reshape(8, 128).T
    return out


def _common_consts(inp):
    ident = np.eye(128, dtype=np.float32).astype(ml_dtypes.bfloat16)
    ones = np.ones((128, 128), np.float32).astype(ml_dtypes.bfloat16)
    maskT = np.triu(np.ones((128, 128), np.float32))
    gc = _gcols([inp["norm_ffn1"][0], inp["norm_mix"][0], inp["norm_ffn2"][0],
                 inp["norm_ffn1"][1], inp["norm_mix"][1], inp["norm_ffn2"][1], inp["final_norm"]])
    epsc = np.full((128, 16), EPS, np.float32)
    return {"ident": ident, "ones": ones, "maskT": maskT, "gcols": gc, "epsc": epsc}


def _core_tokens(inp, c):
    b, s = c // 4, c % 4
    return np.concatenate([inp["meta_tokens"], inp["x"][b, s * SEG:(s + 1) * SEG]], axis=0)


def _slices512(w):
    ns = w.shape[1] // 512
    return np.ascontiguousarray(w.reshape(8, 128, ns, 512).transpose(2, 1, 0, 3))


def _wout_layout(w):
    nft = w.shape[0] // 128
    return np.ascontiguousarray(w.reshape(nft, 128, 8, 128).transpose(2, 1, 0, 3))


_DBG = {}
_LG = [np.log1p(-2.0 ** (-5.0 - h)) for h in range(4)]


def _rot_tables(s):
    pos = np.concatenate([np.arange(NMETA), NMETA + SEG * s + np.arange(SEG)]).astype(np.float32)
    iloc = np.concatenate([np.arange(NMETA), np.arange(SEG) % 128]).astype(np.float64)
    inv = (np.float32(1.0) / (np.float32(10000.0) ** np.linspace(0.0, 1.0, 128, dtype=np.float32))).astype(np.float32)
    ang = (pos[:, None] * inv[None, :]).astype(np.float32)
    cos = np.cos(ang).astype(np.float32).astype(np.float64)
    sin = np.sin(ang).astype(np.float32).astype(np.float64)
    rot = np.zeros((4, 4, 128, NT), np.float32)
    for h in range(4):
        gq = np.exp(_LG[h] * (iloc + 1.0))
        gk = np.exp(-_LG[h] * (iloc + 1.0)) * (256.0 ** -0.5)
        if s > 0:
            gk[:NMETA] = 0.0
        rot[h, 0] = (cos * gq[:, None]).T
        rot[h, 1] = (sin * gq[:, None]).T
        rot[h, 2] = (cos * gk[:, None]).T
        rot[h, 3] = (sin * gk[:, None]).T
    return rot


def _core_consts(s):
    coef = np.zeros((128, 12), np.float32)
    sf = np.zeros((128, 12), np.float32)
    for r in range(3):
        for h in range(4):
            if r < s:
                coef[:, r * 4 + h] = np.exp(_LG[h] * SEG * (s - 1 - r))
        for r2 in range(3):
            if r < r2 < s:
                sf[:, r * 3 + r2] = 1.0
        if r < s:
            sf[:, 9 + r] = 1.0
    return coef, sf


def run_stage(stage, in_maps):
    b = Builder(stage)
    nc = b.build()
    maps = []
    for m in in_maps:
        maps.append({k: m[k] for k in b.inputs})
    res = run_bass_kernel_spmd(nc, maps, core_ids=list(range(8)))
    return res.results


def kernel(**inputs):
    inp = {k: np.asarray(v) for k, v in inputs.items()}
    base = _common_consts(inp)
    ffw = [(inp["ffn1_w_in"][0], inp["ffn1_w_out"][0]), (inp["ffn2_w_in"][0], inp["ffn2_w_out"][0]),
           (inp["ffn1_w_in"][1], inp["ffn1_w_out"][1]), (inp["ffn2_w_in"][1], inp["ffn2_w_out"][1])]
    for i, (a, b_) in enumerate(ffw):
        base["w1_%d" % i] = _w1_layout(a)
        base["w2_%d" % i] = _w2_layout(b_)
    base["ret_win"] = _slices512(inp["ret_w_in"][0])
    base["ret_wout"] = _wout_layout(inp["ret_w_out"][0])
    base["ret_hnB"] = np.ascontiguousarray(np.broadcast_to(inp["ret_head_norm"][0].reshape(1, 2048), (128, 2048)))
    gw = inp["gla_w_in"][0]
    base["gla_win"] = _slices512(gw[:, :3072])
    base["gla_wz"] = np.ascontiguousarray(gw[:, 3072:3088].reshape(8, 128, 16).transpose(1, 0, 2))
    base["gla_wgate"] = np.ascontiguousarray(inp["gla_w_gate"][0])
    base["gla_wout"] = _wout_layout(inp["gla_w_out"][0])
    base["gla_hnB"] = np.ascontiguousarray(np.broadcast_to(inp["gla_head_norm"][0].reshape(1, 1024), (128, 1024)))
    rmask = np.ones((128, 512), np.float32)
    rmask[:, ::128] = 0.0
    base["gla_rmask"] = rmask
    rots = [_rot_tables(s) for s in range(4)]
    maps = []
    for c in range(8):
        b, s = c // 4, c % 4
        m = dict(base)
        m["xT"] = _fm(_core_tokens(inp, c))
        m["rot"] = rots[s]
        coef, sf = _core_consts(s)
        m["ret_coef"] = coef
        m["gla_selflag"] = sf
        gc = np.zeros((128, 16), np.float32)
        gc[:, 0:4] = inp["gla_b_gate"][0].reshape(4, 128).T
        gc[:, 4] = 1.0 if s == 0 else 0.0
        gc[:, 5] = np.log(128.0 ** -0.5)
        gc[:, 6] = 1.0
        m["gla_gcst"] = gc
        maps.append(m)
    r1 = run_stage("l1", maps)
    for c in range(8):
        b = c // 4
        maps[c]["hin"] = r1[c]["hout"]
        maps[c]["ret_Lg"] = np.stack([r1[4 * b + r]["ret_L"].reshape(128, 8, 512) for r in range(3)])
    r2 = run_stage("l2", maps)
    for c in range(8):
        b = c // 4
        maps[c]["hin"] = r2[c]["hout"]
        maps[c]["gla_Lg"] = np.stack([r2[4 * b + r]["gla_L"].reshape(128, 4, 256) for r in range(3)])
        maps[c]["gla_Bg"] = np.stack([r2[4 * b + r]["gla_B"] for r in range(3)])
    r3 = run_stage("l3", maps)
    out = np.zeros((2, 4 * SEG, D), np.float32)
    for c in range(8):
        b, s = c // 4, c % 4
        out[b, s * SEG:(s + 1) * SEG] = r3[c]["outT"].transpose(2, 1, 0).reshape(SEG, D)
    _DBG["r1"], _DBG["r2"] = r1, r2
    return out
```

```python
import os
import numpy as np
import ml_dtypes
import concourse.bass as bass
import concourse.mybir as mybir
from concourse.bass_utils import run_bass_kernel_spmd

F32 = mybir.dt.float32
BF16 = mybir.dt.bfloat16
AF = mybir.ActivationFunctionType
ALU = mybir.AluOpType

D = 1024
NMETA = 16
SEG = 2048
NT = NMETA + SEG
DFF = 2816
NFT = DFF // 128
EPS = 1e-6
GROUPS = [(0, 16)] + [(16 + 512 * i, 16 + 512 * (i + 1)) for i in range(4)]
ENGS = ("pe", "act", "dve", "pool", "sp")
SKIP = set(os.environ.get("K_SKIP", "").split(","))


class Sched:
    def __init__(self, nc):
        self.nc = nc
        self.q = {e: [] for e in ENGS}
        self.cnt = {e: 0 for e in ENGS}
        self.known = {e: {} for e in ENGS}
        self.lastw = {}
        self.readers = {}
        self.dmacnt = {}
        self.semnames = ["p_" + e for e in ENGS]
        self.pending_nobump = {e: False for e in ENGS}

    def _deps(self, eng, reads, writes):
        deps = {}

        def add(tok):
            if tok is None:
                return
            s, v = tok
            if deps.get(s, 0) < v:
                deps[s] = v
        for r in reads:
            add(self.lastw.get(r))
        for w in writes:
            add(self.lastw.get(w))
            for t in self.readers.get(w, ()):
                add(t)
        waits = []
        own = "p_" + eng
        for s, v in deps.items():
            if s == own and eng == "pe":
                continue
            if self.known[eng].get(s, 0) >= v:
                continue
            self.known[eng][s] = v
            waits.append((s, v))
        return waits

    def _record(self, tok, reads, writes):
        for r in reads:
            self.readers.setdefault(r, []).append(tok)
        for w in writes:
            self.lastw[w] = tok
            self.readers[w] = []

    def op(self, eng, fn, reads=(), writes=(), bump=True):
        waits = self._deps(eng, reads, writes)
        own = "p_" + eng
        if bump:
            self.cnt[eng] += 1
            tok = (own, self.cnt[eng])
            inc = (own, 1)
            self.pending_nobump[eng] = False
        else:
            tok = (own, self.cnt[eng] + 1)
            inc = None
            self.pending_nobump[eng] = True
        self.q[eng].append((fn, waits, inc))
        self._record(tok, reads, writes)
        return tok

    def dma(self, eng, fn, sem, reads=(), writes=()):
        if sem not in self.dmacnt:
            self.dmacnt[sem] = 0
            self.semnames.append(sem)
        waits = self._deps(eng, reads, writes)
        self.dmacnt[sem] += 16
        tok = (sem, self.dmacnt[sem])
        self.q[eng].append((fn, waits, (sem, 16)))
        self._record(tok, reads, writes)
        return tok

    def dma_group_end(self, sem, resources):
        tok = (sem, self.dmacnt[sem])
        for r in resources:
            self.lastw[r] = tok

    def barrier(self):
        toks = [("p_" + e, self.cnt[e]) for e in ENGS if self.cnt[e] > 0]
        toks += [(s, v) for s, v in self.dmacnt.items() if v > 0]
        for e in ENGS:
            waits = []
            for s, v in toks:
                if s == "p_" + e and e == "pe":
                    continue
                if self.known[e].get(s, 0) >= v:
                    continue
                self.known[e][s] = v
                waits.append((s, v))
            if waits:
                self.q[e].append((None, waits, None))

    def final_wait(self, eng, toks):
        waits = []
        for s, v in toks:
            if self.known[eng].get(s, 0) >= v:
                continue
            self.known[eng][s] = v
            waits.append((s, v))
        self.q[eng].append((None, waits, None))

    def simulate(self):
        sem = {n: 0 for n in self.semnames}
        pc = {e: 0 for e in ENGS}
        progress = True
        while progress:
            progress = False
            for e in ENGS:
                while pc[e] < len(self.q[e]):
                    fn, waits, inc = self.q[e][pc[e]]
                    if any(sem[s] < v for s, v in waits):
                        break
                    if inc is not None:
                        sem[inc[0]] += inc[1]
                    pc[e] += 1
                    progress = True
        stuck = {e: (pc[e], len(self.q[e]), self.q[e][pc[e]][1]) for e in ENGS if pc[e] < len(self.q[e])}
        if stuck:
            raise RuntimeError("deadlock in schedule: %r ; sems=%r" % (stuck, sem))
        return {e: len(self.q[e]) for e in ENGS}

    def emit(self):
        nc = self.nc
        for e in ENGS:
            assert not self.pending_nobump[e], e
        print("sched sizes", self.simulate(), "nsems", len(self.semnames))
        from contextlib import ExitStack
        with ExitStack() as st:
            sems = {n: st.enter_context(nc.semaphore(n)) for n in self.semnames}
            block = st.enter_context(nc.Block())
            handles = {"pe": block.tensor, "act": block.scalar, "dve": block.vector,
                       "pool": block.gpsimd, "sp": block.sync}

            def mk(ename):
                def body(e):
                    for fn, waits, inc in self.q[ename]:
                        for s, v in waits:
                            e.wait_ge(sems[s], v)
                        if fn is None:
                            continue
                        ins = fn(e)
                        if inc is not None:
                            ins.then_inc(sems[inc[0]], inc[1])
                return body
            for ename in ENGS:
                handles[ename](mk(ename))


class Arena:
    def __init__(self, big, nbytes):
        self.big = big
        self.nbytes = nbytes
        self.off = 0
        self.marks = []

    def alloc(self, nbytes, dtype, shape=None):
        rb = (nbytes + 63) // 64 * 64
        assert self.off + rb <= self.nbytes, (self.off, rb, self.nbytes)
        a = self.big[:, self.off // 2:(self.off + nbytes) // 2]
        self.off += rb
        if dtype is not BF16:
            a = a.bitcast(dtype)
        return a

    def mark(self):
        self.marks.append(self.off)

    def release(self):
        self.off = self.marks.pop()


def v3(ap, a):
    return ap.rearrange("p (a b) -> p a b", a=a)


class Builder:
    def __init__(self, stage):
        self.stage = stage
        self.nc = bass.Bass("TRN2", target_bir_lowering=False)
        self.S = Sched(self.nc)
        self.inputs = {}
        self.outputs = {}
        self.psum_rr = 0
        self.uid = 0
        self.dbgtoks = []

    def din(self, name, shape, dtype=F32):
        t = self.nc.dram_tensor(name, list(shape), dtype, kind="ExternalInput").ap()
        self.inputs[name] = t
        return t

    def dout(self, name, shape, dtype=F32):
        t = self.nc.dram_tensor(name, list(shape), dtype, kind="ExternalOutput").ap()
        self.outputs[name] = t
        return t

    def newid(self, p="r"):
        self.uid += 1
        return "%s%d" % (p, self.uid)

    def dbg(self, name, ap, shape, dtype, reads):
        if "dump" not in SKIP:
            return
        o = self.dout("dbg_" + name, shape, dtype)
        self.dbgtoks.append(self.S.dma("sp", lambda e: e.dma_start(out=o, in_=ap), "dbg_" + name, reads, ["dbg_" + name]))

    def ps(self, dtype=F32):
        b = self.psum_rr
        self.psum_rr = (self.psum_rr + 1) % 8
        ap = self.psum[:, b * 512:(b + 1) * 512]
        if dtype is BF16:
            ap = ap.bitcast(BF16)
        return ap, ("ps", b)

    def mm(self, out, lhsT, rhs, start, stop, reads, writes, bump=None):
        if bump is None:
            bump = stop
        self.S.op("pe", lambda e: e.matmul(out, lhsT, rhs, start=start, stop=stop), reads, writes, bump=bump)

    def tr(self, out, in_, ident, reads, writes, bump=True):
        self.S.op("pe", lambda e: e.transpose(out, in_, ident), reads, writes, bump=bump)

    def act(self, out, in_, func, reads, writes, bias=None, scale=None, accum_out=None):
        kw = {}
        if bias is not None:
            kw["bias"] = bias
        if scale is not None:
            kw["scale"] = scale
        if accum_out is not None:
            kw["accum_out"] = accum_out
        self.S.op("act", lambda e: e.activation(out, in_, func, **kw), reads, writes)

    def stt(self, eng, out, in0, scalar, in1, op0, op1, reads, writes):
        self.S.op(eng, lambda e: e.scalar_tensor_tensor(out, in0, scalar, in1, op0, op1), reads, writes)

    def tt(self, eng, out, in0, in1, op, reads, writes):
        self.S.op(eng, lambda e: e.tensor_tensor(out, in0, in1, op), reads, writes)

    def ts(self, eng, out, in0, s1, s2, op0, op1, reads, writes):
        if op1 is None:
            self.S.op(eng, lambda e: e.tensor_scalar(out, in0, s1, None, op0), reads, writes)
        else:
            self.S.op(eng, lambda e: e.tensor_scalar(out, in0, s1, s2, op0, op1), reads, writes)

    def cp(self, eng, out, in_, reads, writes):
        if eng == "act":
            self.S.op("act", lambda e: e.activation(out, in_, AF.Copy), reads, writes)
        else:
            self.S.op(eng, lambda e: e.tensor_copy(out, in_), reads, writes)

    def wload(self, dst, src, sem, reads, writes, chunk=4096):
        self.S.dma("pool", lambda e: e.dma_start(out=dst, in_=src, max_dma_last_dim=chunk), sem, reads, writes)

    def load(self, dst, src, sem, reads, writes, eng="sp"):
        self.S.dma(eng, lambda e: e.dma_start(out=dst, in_=src), sem, reads, writes)

    def build(self):
        nc = self.nc
        from contextlib import ExitStack
        with ExitStack() as st:
            SB_BYTES = 204 * 1024
            big = st.enter_context(nc.sbuf_tensor("big", [128, SB_BYTES // 2], BF16))
            self.psum = st.enter_context(nc.psum_tensor("psum", [128, 8 * 512], F32))
            self.A = Arena(big, SB_BYTES)
            self.program()
            self.S.emit()
        return nc

    def consts(self):
        A = self.A
        self.ident = A.alloc(128 * 2, BF16)
        self.ones = A.alloc(128 * 2, BF16)
        self.maskT = A.alloc(128 * 4, F32)
        self.gcols = A.alloc(7 * 8 * 4, F32)
        self.epsc = A.alloc(64, F32)
        d_ident = self.din("ident", [128, 128], BF16)
        d_ones = self.din("ones", [128, 128], BF16)
        d_mask = self.din("maskT", [128, 128])
        d_g = self.din("gcols", [128, 56])
        d_eps = self.din("epsc", [128, 16])
        self.load(self.ident, d_ident, "c_ld", [], ["ident"])
        self.load(self.ones, d_ones, "c_ld", [], ["ones"])
        self.load(self.maskT, d_mask, "c_ld", [], ["maskT"])
        self.load(self.gcols, d_g, "c_ld", [], ["gcols"])
        self.load(self.epsc, d_eps, "c_ld", [], ["epsc"])
        self.S.dma_group_end("c_ld", ["ident", "ones", "maskT", "gcols", "epsc"])

    def gcol(self, which, k):
        i = which * 8 + k
        return self.gcols[:, i:i + 1]

    def norm_group(self, g, which, xn_view, xn_res, sq, rstd, sqres=("sq",)):
        t0, t1 = GROUPS[g]
        n = t1 - t0
        sqres = list(sqres)
        hres = [("h", k, g) for k in range(8)]
        sqv = v3(sq, 8)[:, :, :n]
        self.act(sqv, self.hT[:, :, t0:t1], AF.Square, hres, sqres)
        ps, pr = self.ps()
        for k in range(8):
            self.mm(ps[:, :n], self.ones, v3(sq, 8)[:, k, :n], k == 0, k == 7, sqres + ["ones"], [pr])
        self.act(rstd[:, :n], ps[:, :n], AF.Sqrt, [pr, "epsc"], ["rstd"], bias=self.epsc[:, 0:1], scale=1.0 / D)
        self.S.op("dve", lambda e: e.reciprocal(rstd[:, :n], rstd[:, :n]), ["rstd"], ["rstd"])
        for k in range(8):
            eng = "dve"
            self.stt(eng, xn_view[:, k, :n], self.hT[:, k, t0:t1], self.gcol(which, k), rstd[:, :n],
                     ALU.mult, ALU.mult, [("h", k, g), "rstd", "gcols"], [(xn_res, k)])

    def ffn(self, idx, which_norm, w1_d, w2_d):
        A = self.A
        A.mark()
        xn = A.alloc(8 * NT * 2, BF16)
        xn3 = v3(xn, 8)
        actb = A.alloc(11 * NT * 2, BF16)
        act3 = v3(actb, 11)
        w1s = [A.alloc(8 * 256 * 2, BF16) for _ in range(3)]
        w2 = A.alloc(11 * 1024 * 2, BF16)
        w23 = v3(w2, 11)
        rstd = A.alloc(512 * 4, F32)
        stmp = [A.alloc(512 * 4, F32) for _ in range(2)]
        sq = A.alloc(8 * 512 * 2, BF16)
        tag = "f%d" % idx

        def load_w1(j):
            slot = j % 3
            self.wload(w1s[slot], w1_d[j].rearrange("p k c -> p (k c)"), "w1s%d" % slot, [], [("w1", slot)])

        load_w1(0)
        load_w1(1)
        for jj in range(11):
            self.wload(w23[:, jj, :], w2_d[jj], "w2s", [], [("w2", jj)])
        self.S.dma_group_end("w2s", [("w2", jj) for jj in range(11)])
        for g in range(5):
            t0, t1 = GROUPS[g]
            if "norm" in SKIP:
                continue
            self.norm_group(g, which_norm, xn3[:, :, t0:t1], (tag + "xn", g), sq, rstd)
        cnt = 0
        for half in range(2):
            if half == 1:
                for jj in range(11):
                    self.wload(w23[:, jj, :], w2_d[11 + jj], "w2s", [], [("w2", jj)])
                self.S.dma_group_end("w2s", [("w2", jj) for jj in range(11)])
            for jj in range(11):
                j = half * 11 + jj
                slot = j % 3
                w13 = v3(w1s[slot], 8)
                for g in range(5):
                    if "p1" in SKIP:
                        continue
                    t0, t1 = GROUPS[g]
                    n = t1 - t0
                    pg, rg = self.ps()
                    pu, ru = self.ps()
                    xr = [((tag + "xn", g), k) for k in range(8)]
                    for k in range(8):
                        self.mm(pg[:, :n], w13[:, k, 0:128], xn3[:, k, t0:t1], k == 0, k == 7,
                                [("w1", slot), xr[k]], [rg])
                    for k in range(8):
                        self.mm(pu[:, :n], w13[:, k, 128:256], xn3[:, k, t0:t1], k == 0, k == 7,
                                [("w1", slot), xr[k]], [ru])
                    tmp = stmp[cnt % 2]
                    tr_ = ("stmp", cnt % 2)
                    cnt += 1
                    self.act(tmp[:, :n], pg[:, :n], AF.Silu, [rg], [tr_])
                    self.tt("dve", act3[:, jj, t0:t1], pu[:, :n], tmp[:, :n], ALU.mult, [ru, tr_], [("act", jj, g)])
                if j + 2 < NFT:
                    load_w1(j + 2)
            for g in range(5):
                if "p2" in SKIP:
                    continue
                t0, t1 = GROUPS[g]
                n = t1 - t0
                for dt in range(8):
                    po, ro = self.ps()
                    for jj in range(11):
                        self.mm(po[:, :n], w23[:, jj, dt * 128:(dt + 1) * 128], act3[:, jj, t0:t1], jj == 0, jj == 10,
                                [("w2", jj), ("act", jj, g)], [ro])
                    self.stt("dve", self.hT[:, dt, t0:t1], po[:, :n], 0.5, self.hT[:, dt, t0:t1], ALU.mult, ALU.add,
                             [ro, ("h", dt, g)], [("h", dt, g)])
        self.S.barrier()
        A.release()


    def mixer(self, kind, passB, which_norm, d):
        A = self.A
        A.mark()
        ret = kind == "ret"
        H = 4
        ndt = 2 if ret else 1
        dk = 128 * ndt
        dv = 512 if ret else 256
        nqt = H * ndt
        nvt = dv // 128
        nft = H * nvt
        tg = kind + ("B" if passB else "A")
        gam = [1.0 - 2.0 ** (-5.0 - h) for h in range(H)]
        hn = A.alloc(8 * 512 * 2, BF16); hn3 = v3(hn, 8)
        qT = A.alloc(nqt * 512 * 2, BF16); qT3 = v3(qT, nqt)
        kT = A.alloc(nqt * 512 * 2, BF16); kT3 = v3(kT, nqt)
        kTM = A.alloc(4 * H * dk * 2, BF16); kTM3 = v3(kTM, 4)
        vTM = A.alloc(4 * dv * 2, BF16); vTM3 = v3(vTM, 4)
        gw = A.alloc(4 * dv * 2, BF16); gw3 = v3(gw, 4)
        yTM = A.alloc(4 * dv * 2, BF16); yTM3 = v3(yTM, 4)
        scr = A.alloc(16 * 512 * 2, BF16)
        yT3 = v3(scr[:, :nft * 512], nft)
        sq = scr[:, :8 * 512]
        Ltmp = scr.bitcast(F32)[:, :nqt * dv]
        Ltmp3 = v3(Ltmp, nqt)
        X = A.alloc(nqt * dv * 4, F32); X3 = v3(X, nqt)
        Sbf = A.alloc(nqt * dv * 2, BF16); Sbf3 = v3(Sbf, nqt)
        wm = [A.alloc(8 * 512 * 2, BF16) for _ in range(2)]
        wo = [A.alloc(nft * 128 * 2, BF16) for _ in range(1)]
        rstd = A.alloc(512 * 4, F32)
        tmps = [A.alloc(512 * 4, F32) for _ in range(4)]
        gtmp = tmps[2]
        junk = tmps[3]
        scr_res = [(tg + "yT", h_) for h_ in range(H)]
        coef_sb = A.alloc(12 * 4, F32)
        PTb = A.alloc(128 * 2, BF16)
        small = A.alloc(64 * 4, F32)
        ssq = small[:, 0:1]
        rso = small[:, 1:2]
        hnB = A.alloc(H * dv * 4, F32)
        self.load(hnB, d["hnormB"], tg + "_c", [], [tg + "hnB"])
        if ret:
            tabs = A.alloc(4 * 512 * 4, F32); tabs3 = v3(tabs, 4)
        else:
            wz = A.alloc(8 * 16 * 2, BF16); wz3 = v3(wz, 8)
            wgt = A.alloc(512 * 2, BF16)
            gcst = A.alloc(16 * 4, F32)
            rmask = A.alloc(512 * 4, F32)
            zTb = A.alloc(512 * 2, BF16)
            bcum = A.alloc(H * 512 * 4, F32); bcum3 = v3(bcum, H)
            Eq = A.alloc(H * 512 * 4, F32); Eq3 = v3(Eq, H)
            Ek = A.alloc(H * 512 * 4, F32); Ek3 = v3(Ek, H)
            dcols = A.alloc(H * 20 * 4, F32); dcols3 = v3(dcols, H)
            Bacc = A.alloc(4 * 4, F32)
            self.wload(wz, d["wz"].rearrange("p k c -> p (k c)"), tg + "_wz", [], [tg + "wz"])
            self.wload(wgt[0:16, :], d["wgate"], tg + "_wg", [], [tg + "wgt"])
            self.load(gcst, d["gcst"], tg + "_c2", [], [tg + "gcst"])
            self.ts("dve", gcst[:, 0:4], gcst[:, 0:4], -1.0, None, ALU.mult, None, [tg + "gcst"], [tg + "gcst"])
            self.load(rmask, d["rmask"], tg + "_c3", [], [tg + "rmask"])
            self.S.op("dve", lambda e: e.memset(Bacc, 0.0), [], [tg + "Bacc"])
        self.S.op("dve", lambda e: e.memset(X, 0.0), [], [(tg + "X", j) for j in range(nqt)])
        self.S.op("pool", lambda e: e.memset(Sbf, 0.0), [], [(tg + "S", j) for j in range(nqt)])
        if passB:
            if ret:
                self.load(coef_sb, d["coef"], tg + "_cf", [], [tg + "coef"])
            else:
                Bsb = A.alloc(12 * 4, F32)
                sf = A.alloc(12 * 4, F32)
                cum = A.alloc(12 * 4, F32)
                self.load(v3(Bsb, 3), d["Bg"].rearrange("r p h -> p r h"), tg + "_cf", [], [tg + "Bsb"])
                self.load(sf, d["selflag"], tg + "_cf2", [], [tg + "sf"])
                for r in range(3):
                    cr = cum[:, r * 4:(r + 1) * 4]
                    self.ts("dve", cr, Bsb[:, 0:4], sf[:, r * 3:r * 3 + 1], None, ALU.mult, None, [tg + "Bsb", tg + "sf"], [tg + "cum"])
                    for r2 in (1, 2):
                        self.stt("dve", cr, Bsb[:, r2 * 4:(r2 + 1) * 4], sf[:, r * 3 + r2:r * 3 + r2 + 1], cr, ALU.mult, ALU.add,
                                 [tg + "Bsb", tg + "sf", tg + "cum"], [tg + "cum"])
                self.act(coef_sb, cum, AF.Exp, [tg + "cum"], [tg + "coef"])
                for r in range(3):
                    self.ts("dve", coef_sb[:, r * 4:(r + 1) * 4], coef_sb[:, r * 4:(r + 1) * 4], sf[:, 9 + r:10 + r], None,
                            ALU.mult, None, [tg + "coef", tg + "sf"], [tg + "coef"])
        dX = [1.0] * H
        win = d["win"]
        wslot = [None, None]
        wcnt = [0]

        def wslice(i):
            for s in range(2):
                if wslot[s] == i:
                    return v3(wm[s], 8), (tg + "wm", s)
            s = wcnt[0] % 2
            wcnt[0] += 1
            wslot[s] = i
            self.wload(wm[s], win[i].rearrange("p k c -> p (k c)"), tg + "_wm%d" % s, [], [(tg + "wm", s)])
            return v3(wm[s], 8), (tg + "wm", s)

        chunk_global = 0
        for g in range(5):
            t0, t1 = GROUPS[g]
            n = t1 - t0
            C = 16 if g == 0 else 128
            nch = n // C
            self.norm_group(g, which_norm, hn3[:, :, :n], tg + "hn", sq[:, :], rstd, sqres=scr_res)
            hres = [(tg + "hn", k) for k in range(8)]
            if not ret:
                pz, rz = self.ps()
                for k in range(8):
                    self.mm(pz[:16, :n], wz3[:, k, :], hn3[:, k, :n], k == 0, k == 7, [tg + "wz", hres[k]], [rz])
                self.cp("act", zTb[0:16, :n], pz[:16, :n], [rz], [tg + "zT"])
                for h in range(H):
                    pl, rl = self.ps()
                    self.mm(pl[:, :n], wgt[0:16, h * 128:(h + 1) * 128], zTb[0:16, :n], True, True, [tg + "wgt", tg + "zT"], [rl])
                    self.act(tmps[0][:, :n], pl[:, :n], AF.Exp, [rl, tg + "gcst"], ["tmp0"], bias=gcst[:, h:h + 1], scale=-1.0)
                    self.act(tmps[1][:, :n], tmps[0][:, :n], AF.Ln, ["tmp0", tg + "gcst"], ["tmp1"], bias=gcst[:, 6:7], scale=1.0)
                    rm = rmask[:, 1:17] if g == 0 else rmask[:, :n]
                    self.S.op("dve", (lambda o=bcum3[:, h, :n], a=rm, b=tmps[1][:, :n]:
                                      (lambda e: e.tensor_tensor_scan(o, a, b, 0.0, ALU.mult, ALU.add)))(),
                              ["tmp1", tg + "rmask"], [(tg + "bc", h)])
                    self.act(Eq3[:, h, :n], bcum3[:, h, :n], AF.Exp, [(tg + "bc", h), tg + "gcst"], [(tg + "Eq", h)],
                             bias=gcst[:, 5:6], scale=-1.0 / 16.0)
                    self.act(Ek3[:, h, :n], bcum3[:, h, :n], AF.Exp, [(tg + "bc", h)], [(tg + "Ek", h)], scale=1.0 / 16.0)
                    if g == 0:
                        self.ts("dve", Ek3[:, h, :n], Ek3[:, h, :n], gcst[:, 4:5], None, ALU.mult, None,
                                [(tg + "Ek", h), tg + "gcst"], [(tg + "Ek", h)])
                    lastcols = bcum3[:, h, C - 1:n:C]
                    self.act(dcols3[:, h, chunk_global:chunk_global + nch], lastcols, AF.Exp, [(tg + "bc", h)],
                             [(tg + "dc", h)], scale=-1.0 / 16.0)
                    if (not passB) and g > 0:
                        for c in range(nch):
                            col = bcum3[:, h, c * C + C - 1:c * C + C]
                            self.tt("dve", Bacc[:, h:h + 1], Bacc[:, h:h + 1], col, ALU.add, [(tg + "bc", h), tg + "Bacc"], [tg + "Bacc"])
            kinds = ("q", "k") if passB else ("k",)
            for h in range(H):
                for qk in kinds:
                    dstT = qT3 if qk == "q" else kT3
                    base = 0 if qk == "q" else nqt
                    pss = []
                    for dt in range(ndt):
                        j = base + h * ndt + dt
                        wv_, wr = wslice(j // 4)
                        pp, pr = self.ps()
                        for k in range(8):
                            self.mm(pp[:, :n], wv_[:, k, (j % 4) * 128:(j % 4 + 1) * 128], hn3[:, k, :n], k == 0, k == 7,
                                    [wr, hres[k]], [pr])
                        pss.append((pp, pr))
                    dres = [(tg + qk + "T", h * ndt + dt) for dt in range(ndt)]
                    if ret:
                        if qk == kinds[0]:
                            self.S.dma("sp", (lambda h=h, t0=t0, t1=t1, n=n: (lambda e: e.dma_start(
                                out=tabs3[:, :, :n], in_=d["rot"][h, :, :, t0:t1].rearrange("a p t -> p a t"))))(),
                                tg + "_tab", [], [tg + "tabs"])
                        ci, si = (0, 1) if qk == "q" else (2, 3)
                        (pa, ra), (pb, rb) = pss
                        cT = tabs3[:, ci, :n]
                        sT = tabs3[:, si, :n]
                        tb = tg + "tabs"
                        self.tt("dve", tmps[0][:, :n], pa[:, :n], cT, ALU.mult, [ra, tb], ["tmp0"])
                        self.tt("dve", tmps[1][:, :n], pb[:, :n], sT, ALU.mult, [rb, tb], ["tmp1"])
                        self.tt("pool", dstT[:, h * 2, :n], tmps[0][:, :n], tmps[1][:, :n], ALU.subtract, ["tmp0", "tmp1"], [dres[0]])
                        self.tt("dve", tmps[2][:, :n], pa[:, :n], sT, ALU.mult, [ra, tb], ["tmp2"])
                        self.tt("dve", tmps[3][:, :n], pb[:, :n], cT, ALU.mult, [rb, tb], ["tmp3"])
                        self.tt("pool", dstT[:, h * 2 + 1, :n], tmps[2][:, :n], tmps[3][:, :n], ALU.add, ["tmp2", "tmp3"], [dres[1]])
                    else:
                        E3 = Eq3 if qk == "q" else Ek3
                        er = (tg + ("Eq" if qk == "q" else "Ek"), h)
                        pp, pr = pss[0]
                        self.tt("dve", dstT[:, h, :n], pp[:, :n], E3[:, h, :n], ALU.mult, [pr, er], [dres[0]])
            if passB and (not ret) and g == 1:
                self.dbg("qT", qT, [128, nqt * 512], BF16, [(tg + "qT", j) for j in range(nqt)])
                self.dbg("kT", kT, [128, nqt * 512], BF16, [(tg + "kT", j) for j in range(nqt)])
                self.dbg("Eq", Eq, [128, H * 512], F32, [(tg + "Eq", j) for j in range(H)])
                self.dbg("Ek", Ek, [128, H * 512], F32, [(tg + "Ek", j) for j in range(H)])
                self.dbg("coef", coef_sb, [128, 12], F32, [tg + "coef"])
            for c in range(nch):
                for j0 in range(0, nqt, 4):
                    pt, prt = self.ps(BF16)
                    for jj in range(4):
                        j = j0 + jj
                        self.tr(pt[:C, jj * 128:(jj + 1) * 128], kT3[:, j, c * C:(c + 1) * C], self.ident,
                                [(tg + "kT", j), "ident"], [prt], bump=(jj == 3))
                    self.cp("act", kTM3[:C, c, j0 * 128:(j0 + 4) * 128], pt[:C, :512], [prt], [(tg + "kTM", c)])
            if passB and g == 1:
              for h in range(H):
                  for dt in range(ndt):
                      j = h * ndt + dt
                      self.ts("dve", X3[:, j, :], X3[:, j, :], dX[h], None, ALU.mult, None, [(tg + "X", j)], [(tg + "X", j)])
                  for r in range(3):
                      self.load(Ltmp3[:, h * ndt:(h + 1) * ndt, :], d["Lg"][r][:, h * ndt:(h + 1) * ndt, :], tg + "_lt",
                                [], scr_res)
                      for dt in range(ndt):
                          j = h * ndt + dt
                          cf = coef_sb[:, r * 4 + h:r * 4 + h + 1]
                          self.stt("dve", X3[:, j, :], Ltmp3[:, j, :], cf, X3[:, j, :], ALU.mult, ALU.add,
                                   scr_res + [(tg + "X", j), tg + "coef"], [(tg + "X", j)])
                  for dt in range(ndt):
                      j = h * ndt + dt
                      self.cp("act", Sbf3[:, j, :], X3[:, j, :], [(tg + "X", j)], [(tg + "S", j)])
                  dX[h] = 1.0
            for h in range(H):
                if ret:
                    vsl, voff = 4 + h, 0
                    gsl, goff = 8 + h, 0
                else:
                    vsl, voff = 2 + h // 2, (h % 2) * 256
                    gsl, goff = 4 + h // 2, (h % 2) * 256
                wv_, wr = wslice(vsl)
                for c in range(nch):
                    pv, prv = self.ps()
                    for k in range(8):
                        self.mm(pv[:C, :dv], hn3[:, k, c * C:(c + 1) * C], wv_[:, k, voff:voff + dv], k == 0, k == 7,
                                [wr, hres[k]], [prv])
                    self.cp("act", vTM3[:C, c, :], pv[:C, :dv], [prv], [(tg + "v", c)])
                if passB:
                    wg_, wgr = wslice(gsl)
                    for c in range(nch):
                        pg_, prg = self.ps()
                        for k in range(8):
                            self.mm(pg_[:C, :dv], hn3[:, k, c * C:(c + 1) * C], wg_[:, k, goff:goff + dv], k == 0, k == 7,
                                    [wgr, hres[k]], [prg])
                        self.act(gtmp[:C, :dv], pg_[:C, :dv], AF.Silu, [prg], ["tmp2"])
                        self.tt("pool", gw3[:C, c, :], gtmp[:C, :dv], hnB[:C, h * dv:(h + 1) * dv], ALU.mult,
                                ["tmp2", tg + "hnB"], [(tg + "gw", c)])
                for c in range(nch):
                    cg = chunk_global + c
                    c0, c1 = c * C, (c + 1) * C
                    last_chunk = (g == 4 and c == nch - 1)
                    if passB:
                        psc, rsc = self.ps()
                        for dt in range(ndt):
                            j = h * ndt + dt
                            self.mm(psc[:C, :C], kT3[:, j, c0:c1], qT3[:, j, c0:c1], dt == 0, dt == ndt - 1,
                                    [(tg + "kT", j), (tg + "qT", j)], [rsc])
                        self.tt("dve", PTb[:C, :C], psc[:C, :C], self.maskT[:C, :C], ALU.mult, [rsc, "maskT"], [tg + "PT"])
                        po, ro = self.ps()
                        self.mm(po[:C, :dv], PTb[:C, :C], vTM3[:C, c, :], True, False, [tg + "PT", (tg + "v", c)], [ro], bump=False)
                        for dt in range(ndt):
                            j = h * ndt + dt
                            self.mm(po[:C, :dv], qT3[:, j, c0:c1], Sbf3[:, j, :], False, dt == ndt - 1,
                                    [(tg + "qT", j), (tg + "S", j)], [ro])
                    if not (passB and last_chunk):
                        dc = (gam[h] ** C) if ret else dcols3[:, h, cg:cg + 1]
                        dcr = [] if ret else [(tg + "dc", h)]
                        for dt in range(ndt):
                            j = h * ndt + dt
                            pu, ru = self.ps()
                            self.mm(pu[:, :dv], kTM3[:C, c, j * 128:(j + 1) * 128], vTM3[:C, c, :], True, True,
                                    [(tg + "kTM", c), (tg + "v", c)], [ru])
                            self.stt("dve", X3[:, j, :], X3[:, j, :], dX[h], pu[:, :dv], ALU.mult, ALU.add,
                                     [(tg + "X", j), ru] + dcr, [(tg + "X", j)])
                            if passB:
                                self.ts("dve", Sbf3[:, j, :], X3[:, j, :], dc, None, ALU.mult, None,
                                        [(tg + "X", j)] + dcr, [(tg + "S", j)])
                        dX[h] = dc
                    if passB:
                        self.act(junk[:C, :dv], po[:C, :dv], AF.Square, [ro], ["tmp3", tg + "ssq"], accum_out=ssq[:C, :])
                        self.act(rso[:C, :], ssq[:C, :], AF.Sqrt, [tg + "ssq", "epsc"], [tg + "rso"], bias=self.epsc[:C, 0:1], scale=1.0 / dv)
                        self.S.op("dve", (lambda a=rso[:C, :]: (lambda e: e.reciprocal(a, a)))(), [tg + "rso"], [tg + "rso"])
                        self.stt("dve", yTM3[:C, c, :], po[:C, :dv], rso[:C, :], gw3[:C, c, :], ALU.mult, ALU.mult,
                                 [ro, tg + "rso", (tg + "gw", c)], [(tg + "y", c)])
                if passB:
                    for c in range(nch):
                        pt, prt = self.ps(BF16)
                        for vt in range(nvt):
                            self.tr(pt[:, vt * C:(vt + 1) * C], yTM3[:C, c, vt * 128:(vt + 1) * 128], self.ident[:C, :C],
                                    [(tg + "y", c), "ident"], [prt], bump=(vt == nvt - 1))
                        self.cp("act", yT3[:, h * nvt:(h + 1) * nvt, c * C:(c + 1) * C],
                                pt[:, :nvt * C].rearrange("p (a b) -> p a b", a=nvt), [prt], [(tg + "yT", h)])
            if passB and (not ret) and g == 1:
                self.dbg("yT", scr[:, :nft * 512], [128, nft * 512], BF16, scr_res)
                self.dbg("X", X, [128, nqt * dv], F32, [(tg + "X", j) for j in range(nqt)])
                self.dbg("v3", vTM, [128, 4 * dv], BF16, [(tg + "v", c) for c in range(4)])
                self.dbg("gw3", gw, [128, 4 * dv], BF16, [(tg + "gw", c) for c in range(4)])
                self.dbg("yTM3", yTM, [128, 4 * dv], BF16, [(tg + "y", c) for c in range(4)])
            chunk_global += nch
            if passB:
                for dt in range(8):
                    s = 0
                    self.wload(wo[s], d["wout"][dt].rearrange("p k c -> p (k c)"), tg + "_wo%d" % s, [], [(tg + "wo", s)])
                    wo3 = v3(wo[s], nft)
                    pm, rm_ = self.ps()
                    for ft in range(nft):
                        self.mm(pm[:, :n], wo3[:, ft, :], yT3[:, ft, :n], ft == 0, ft == nft - 1,
                                [(tg + "wo", s), (tg + "yT", ft // nvt)], [rm_])
                    self.stt("dve", self.hT[:, dt, t0:t1], pm[:, :n], 1.0, self.hT[:, dt, t0:t1], ALU.mult, ALU.add,
                             [rm_, ("h", dt, g)], [("h", dt, g)])
        toks = []
        if not passB:
            for h in range(H):
                for dt in range(ndt):
                    j = h * ndt + dt
                    dcr = [] if ret else [(tg + "dc", h)]
                    self.ts("dve", X3[:, j, :], X3[:, j, :], dX[h], None, ALU.mult, None, [(tg + "X", j)] + dcr, [(tg + "X", j)])
            toks.append(self.S.dma("sp", lambda e: e.dma_start(out=d["Lout"], in_=X), tg + "_lo",
                                   [(tg + "X", j) for j in range(nqt)], [tg + "Lout"]))
            if not ret:
                self.ts("dve", Bacc, Bacc, -1.0 / 16.0, None, ALU.mult, None, [tg + "Bacc"], [tg + "Bacc"])
                toks.append(self.S.dma("sp", lambda e: e.dma_start(out=d["Bout"], in_=Bacc), tg + "_bo", [tg + "Bacc"], [tg + "Bout"]))
        self.S.barrier()
        A.release()
        return toks

    def final_norm(self, which, out_d):
        A = self.A
        A.mark()
        sq = A.alloc(8 * 512 * 2, BF16)
        rstd = A.alloc(512 * 4, F32)
        ob = [A.alloc(8 * 512 * 4, F32) for _ in range(2)]
        toks = []
        for g in range(1, 5):
            t0, t1 = GROUPS[g]
            o3 = v3(ob[g % 2], 8)
            self.norm_group(g, which, o3, ("fo", g % 2), sq, rstd)
            tok = self.S.dma("sp", (lambda o3=o3, t0=t0, t1=t1: (lambda e: e.dma_start(out=out_d[:, :, t0 - 16:t1 - 16], in_=o3)))(),
                             "o_st%d" % (g % 2), [(("fo", g % 2), k) for k in range(8)], [("outd", g)])
            toks.append(tok)
        self.S.final_wait("sp", toks)
        A.release()

    def load_h(self, src_d):
        for g in range(5):
            t0, t1 = GROUPS[g]
            self.load(self.hT[:, :, t0:t1], src_d[:, :, t0:t1], "h_ld%d" % g, [], [("h", k, g) for k in range(8)])

    def store_h(self, dst_d):
        toks = []
        for g in range(5):
            t0, t1 = GROUPS[g]
            tok = self.S.dma("sp", (lambda t0=t0, t1=t1: (lambda e: e.dma_start(out=dst_d[:, :, t0:t1], in_=self.hT[:, :, t0:t1])))(),
                             "h_st", [("h", k, g) for k in range(8)], [("hout", g)])
            toks.append(tok)
        return toks

    def program(self):
        A = self.A
        hT = A.alloc(8 * NT * 4, F32)
        self.hT = v3(hT, 8)
        self.consts()
        st = self.stage
        if st == "ffn_only":
            xT = self.din("xT", [128, 8, NT])
            w1 = self.din("w1_0", [NFT, 128, 8, 256])
            w2 = self.din("w2_0", [NFT, 128, 1024])
            hout = self.dout("hout", [128, 8, NT])
            self.load_h(xT)
            self.ffn(0, 0, w1, w2)
            toks = self.store_h(hout)
            self.S.final_wait("sp", toks)
            return

        def ffn_in(i):
            return self.din("w1_%d" % i, [NFT, 128, 8, 256]), self.din("w2_%d" % i, [NFT, 128, 1024])

        def ret_in(passB):
            d = {"win": self.din("ret_win", [12, 128, 8, 512]), "rot": self.din("rot", [4, 4, 128, NT]),
                 "hnormB": self.din("ret_hnB", [128, 2048])}
            if passB:
                d["wout"] = self.din("ret_wout", [8, 128, 16, 128])
                d["Lg"] = self.din("ret_Lg", [3, 128, 8, 512])
                d["coef"] = self.din("ret_coef", [128, 12])
            else:
                d["Lout"] = self.dout("ret_L", [128, 8 * 512])
            return d

        def gla_in(passB):
            d = {"win": self.din("gla_win", [6, 128, 8, 512]), "wz": self.din("gla_wz", [128, 8, 16]),
                 "wgate": self.din("gla_wgate", [16, 512]), "gcst": self.din("gla_gcst", [128, 16]),
                 "rmask": self.din("gla_rmask", [128, 512]), "hnormB": self.din("gla_hnB", [128, 1024])}
            if passB:
                d["wout"] = self.din("gla_wout", [8, 128, 8, 128])
                d["Lg"] = self.din("gla_Lg", [3, 128, 4, 256])
                d["Bg"] = self.din("gla_Bg", [3, 128, 4])
                d["selflag"] = self.din("gla_selflag", [128, 12])
            else:
                d["Lout"] = self.dout("gla_L", [128, 4 * 256])
                d["Bout"] = self.dout("gla_B", [128, 4])
            return d

        toks = []
        if st == "l1":
            self.load_h(self.din("xT", [128, 8, NT]))
            self.ffn(0, 0, *ffn_in(0))
            toks += self.mixer("ret", False, 1, ret_in(False))
            toks += self.store_h(self.dout("hout", [128, 8, NT]))
        elif st == "l2":
            self.load_h(self.din("hin", [128, 8, NT]))
            self.mixer("ret", True, 1, ret_in(True))
            self.ffn(1, 2, *ffn_in(1))
            self.ffn(2, 3, *ffn_in(2))
            toks += self.mixer("gla", False, 4, gla_in(False))
            toks += self.store_h(self.dout("hout", [128, 8, NT]))
        elif st == "l3":
            self.load_h(self.din("hin", [128, 8, NT]))
            self.mixer("gla", True, 4, gla_in(True))
            if "dbg3" in SKIP:
                toks += self.store_h(self.dout("hdbg", [128, 8, NT]))
            self.ffn(3, 5, *ffn_in(3))
            self.final_norm(6, self.dout("outT", [128, 8, SEG]))
        self.S.final_wait("sp", toks + self.dbgtoks)


def _fm(arr):
    T = arr.shape[0]
    return np.ascontiguousarray(arr.reshape(T, 8, 128).transpose(2, 1, 0))


def _w1_layout(w_in):
    g = w_in[:, :DFF].reshape(8, 128, NFT, 128)
    u = w_in[:, DFF:].reshape(8, 128, NFT, 128)
    w = np.concatenate([g, u], axis=3)
    return np.ascontiguousarray(w.transpose(2, 1, 0, 3))


def _w2_layout(w_out):
    return np.ascontiguousarray(w_out.reshape(NFT, 128, D))


def _gcols(vecs):
    out = np.zeros((128, 56), np.float32)
    for i, v in enumerate(vecs):
        out[:, i * 8:(i + 1) * 8] = v.reshape(8, 128).T
    return out


def _common_consts(inp):
    ident = np.eye(128, dtype=np.float32).astype(ml_dtypes.bfloat16)
    ones = np.ones((128, 128), np.float32).astype(ml_dtypes.bfloat16)
    maskT = np.triu(np.ones((128, 128), np.float32))
    gc = _gcols([inp["norm_ffn1"][0], inp["norm_mix"][0], inp["norm_ffn2"][0],
                 inp["norm_ffn1"][1], inp["norm_mix"][1], inp["norm_ffn2"][1], inp["final_norm"]])
    epsc = np.full((128, 16), EPS, np.float32)
    return {"ident": ident, "ones": ones, "maskT": maskT, "gcols": gc, "epsc": epsc}


def _core_tokens(inp, c):
    b, s = c // 4, c % 4
    return np.concatenate([inp["meta_tokens"], inp["x"][b, s * SEG:(s + 1) * SEG]], axis=0)


def _slices512(w):
    ns = w.shape[1] // 512
    return np.ascontiguousarray(w.reshape(8, 128, ns, 512).transpose(2, 1, 0, 3))


def _wout_layout(w):
    nft = w.shape[0] // 128
    return np.ascontiguousarray(w.reshape(nft, 128, 8, 128).transpose(2, 1, 0, 3))


_DBG = {}
_LG = [np.log1p(-2.0 ** (-5.0 - h)) for h in range(4)]


def _rot_tables(s):
    pos = np.concatenate([np.arange(NMETA), NMETA + SEG * s + np.arange(SEG)]).astype(np.float32)
    iloc = np.concatenate([np.arange(NMETA), np.arange(SEG) % 128]).astype(np.float64)
    inv = (np.float32(1.0) / (np.float32(10000.0) ** np.linspace(0.0, 1.0, 128, dtype=np.float32))).astype(np.float32)
    ang = (pos[:, None] * inv[None, :]).astype(np.float32)
    cos = np.cos(ang).astype(np.float32).astype(np.float64)
    sin = np.sin(ang).astype(np.float32).astype(np.float64)
    rot = np.zeros((4, 4, 128, NT), np.float32)
    for h in range(4):
        gq = np.exp(_LG[h] * (iloc + 1.0))
        gk = np.exp(-_LG[h] * (iloc + 1.0)) * (256.0 ** -0.5)
        if s > 0:
            gk[:NMETA] = 0.0
        rot[h, 0] = (cos * gq[:, None]).T
        rot[h, 1] = (sin * gq[:, None]).T
        rot[h, 2] = (cos * gk[:, None]).T
        rot[h, 3] = (sin * gk[:, None]).T
    return rot


def _core_consts(s):
    coef = np.zeros((128, 12), np.float32)
    sf = np.zeros((128, 12), np.float32)
    for r in range(3):
        for h in range(4):
            if r < s:
                coef[:, r * 4 + h] = np.exp(_LG[h] * SEG * (s - 1 - r))
        for r2 in range(3):
            if r < r2 < s:
                sf[:, r * 3 + r2] = 1.0
        if r < s:
            sf[:, 9 + r] = 1.0
    return coef, sf


def run_stage(stage, in_maps):
    b = Builder(stage)
    nc = b.build()
    maps = []
    for m in in_maps:
        maps.append({k: m[k] for k in b.inputs})
    res = run_bass_kernel_spmd(nc, maps, core_ids=list(range(8)))
    return res.results


def kernel(**inputs):
    inp = {k: np.asarray(v) for k, v in inputs.items()}
    base = _common_consts(inp)
    ffw = [(inp["ffn1_w_in"][0], inp["ffn1_w_out"][0]), (inp["ffn2_w_in"][0], inp["ffn2_w_out"][0]),
           (inp["ffn1_w_in"][1], inp["ffn1_w_out"][1]), (inp["ffn2_w_in"][1], inp["ffn2_w_out"][1])]
    for i, (a, b_) in enumerate(ffw):
        base["w1_%d" % i] = _w1_layout(a)
        base["w2_%d" % i] = _w2_layout(b_)
    base["ret_win"] = _slices512(inp["ret_w_in"][0])
    base["ret_wout"] = _wout_layout(inp["ret_w_out"][0])
    base["ret_hnB"] = np.ascontiguousarray(np.broadcast_to(inp["ret_head_norm"][0].reshape(1, 2048), (128, 2048)))
    gw = inp["gla_w_in"][0]
    base["gla_win"] = _slices512(gw[:, :3072])
    base["gla_wz"] = np.ascontiguousarray(gw[:, 3072:3088].reshape(8, 128, 16).transpose(1, 0, 2))
    base["gla_wgate"] = np.ascontiguousarray(inp["gla_w_gate"][0])
    base["gla_wout"] = _wout_layout(inp["gla_w_out"][0])
    base["gla_hnB"] = np.ascontiguousarray(np.broadcast_to(inp["gla_head_norm"][0].reshape(1, 1024), (128, 1024)))
    rmask = np.ones((128, 512), np.float32)
    rmask[:, ::128] = 0.0
    base["gla_rmask"] = rmask
    rots = [_rot_tables(s) for s in range(4)]
    maps = []
    for c in range(8):
        b, s = c // 4, c % 4
        m = dict(base)
        m["xT"] = _fm(_core_tokens(inp, c))
        m["rot"] = rots[s]
        coef, sf = _core_consts(s)
        m["ret_coef"] = coef
        m["gla_selflag"] = sf
        gc = np.zeros((128, 16), np.float32)
        gc[:, 0:4] = inp["gla_b_gate"][0].reshape(4, 128).T
        gc[:, 4] = 1.0 if s == 0 else 0.0
        gc[:, 5] = np.log(128.0 ** -0.5)
        gc[:, 6] = 1.0
        m["gla_gcst"] = gc
        maps.append(m)
    r1 = run_stage("l1", maps)
    for c in range(8):
        b = c // 4
        maps[c]["hin"] = r1[c]["hout"]
        maps[c]["ret_Lg"] = np.stack([r1[4 * b + r]["ret_L"].reshape(128, 8, 512) for r in range(3)])
    r2 = run_stage("l2", maps)
    for c in range(8):
        b = c // 4
        maps[c]["hin"] = r2[c]["hout"]
        maps[c]["gla_Lg"] = np.stack([r2[4 * b + r]["gla_L"].reshape(128, 4, 256) for r in range(3)])
        maps[c]["gla_Bg"] = np.stack([r2[4 * b + r]["gla_B"] for r in range(3)])
    r3 = run_stage("l3", maps)
    out = np.zeros((2, 4 * SEG, D), np.float32)
    for c in range(8):
        b, s = c // 4, c % 4
        out[b, s * SEG:(s + 1) * SEG] = r3[c]["outT"].transpose(2, 1, 0).reshape(SEG, D)
    _DBG["r1"], _DBG["r2"] = r1, r2
    return out
```

```python
import os
import numpy as np
import ml_dtypes
import concourse.bass as bass
import concourse.mybir as mybir
from concourse.bass_utils import run_bass_kernel_spmd

F32 = mybir.dt.float32
BF16 = mybir.dt.bfloat16
AF = mybir.ActivationFunctionType
ALU = mybir.AluOpType

D = 1024
NMETA = 16
SEG = 2048
NT = NMETA + SEG
DFF = 2816
NFT = DFF // 128
EPS = 1e-6
GROUPS = [(0, 16)] + [(16 + 512 * i, 16 + 512 * (i + 1)) for i in range(4)]
ENGS = ("pe", "act", "dve", "pool", "sp")
SKIP = set(os.environ.get("K_SKIP", "").split(","))


class Sched:
    def __init__(self, nc):
        self.nc = nc
        self.q = {e: [] for e in ENGS}
        self.cnt = {e: 0 for e in ENGS}
        self.known = {e: {} for e in ENGS}
        self.lastw = {}
        self.readers = {}
        self.dmacnt = {}
        self.semnames = ["p_" + e for e in ENGS]
        self.pending_nobump = {e: False for e in ENGS}

    def _deps(self, eng, reads, writes):
        deps = {}

        def add(tok):
            if tok is None:
                return
            s, v = tok
            if deps.get(s, 0) < v:
                deps[s] = v
        for r in reads:
            add(self.lastw.get(r))
        for w in writes:
            add(self.lastw.get(w))
            for t in self.readers.get(w, ()):
                add(t)
        waits = []
        own = "p_" + eng
        for s, v in deps.items():
            if s == own and eng == "pe":
                continue
            if self.known[eng].get(s, 0) >= v:
                continue
            self.known[eng][s] = v
            waits.append((s, v))
        return waits

    def _record(self, tok, reads, writes):
        for r in reads:
            self.readers.setdefault(r, []).append(tok)
        for w in writes:
            self.lastw[w] = tok
            self.readers[w] = []

    def op(self, eng, fn, reads=(), writes=(), bump=True):
        waits = self._deps(eng, reads, writes)
        own = "p_" + eng
        if bump:
            self.cnt[eng] += 1
            tok = (own, self.cnt[eng])
            inc = (own, 1)
            self.pending_nobump[eng] = False
        else:
            tok = (own, self.cnt[eng] + 1)
            inc = None
            self.pending_nobump[eng] = True
        self.q[eng].append((fn, waits, inc))
        self._record(tok, reads, writes)
        return tok

    def dma(self, eng, fn, sem, reads=(), writes=()):
        if sem not in self.dmacnt:
            self.dmacnt[sem] = 0
            self.semnames.append(sem)
        waits = self._deps(eng, reads, writes)
        self.dmacnt[sem] += 16
        tok = (sem, self.dmacnt[sem])
        self.q[eng].append((fn, waits, (sem, 16)))
        self._record(tok, reads, writes)
        return tok

    def dma_group_end(self, sem, resources):
        tok = (sem, self.dmacnt[sem])
        for r in resources:
            self.lastw[r] = tok

    def barrier(self):
        toks = [("p_" + e, self.cnt[e]) for e in ENGS if self.cnt[e] > 0]
        toks += [(s, v) for s, v in self.dmacnt.items() if v > 0]
        for e in ENGS:
            waits = []
            for s, v in toks:
                if s == "p_" + e and e == "pe":
                    continue
                if self.known[e].get(s, 0) >= v:
                    continue
                self.known[e][s] = v
                waits.append((s, v))
            if waits:
                self.q[e].append((None, waits, None))

    def final_wait(self, eng, toks):
        waits = []
        for s, v in toks:
            if self.known[eng].get(s, 0) >= v:
                continue
            self.known[eng][s] = v
            waits.append((s, v))
        self.q[eng].append((None, waits, None))

    def simulate(self):
        sem = {n: 0 for n in self.semnames}
        pc = {e: 0 for e in ENGS}
        progress = True
        while progress:
            progress = False
            for e in ENGS:
                while pc[e] < len(self.q[e]):
                    fn, waits, inc = self.q[e][pc[e]]
                    if any(sem[s] < v for s, v in waits):
                        break
                    if inc is not None:
                        sem[inc[0]] += inc[1]
                    pc[e] += 1
                    progress = True
        stuck = {e: (pc[e], len(self.q[e]), self.q[e][pc[e]][1]) for e in ENGS if pc[e] < len(self.q[e])}
        if stuck:
            raise RuntimeError("deadlock in schedule: %r ; sems=%r" % (stuck, sem))
        return {e: len(self.q[e]) for e in ENGS}

    def emit(self):
        nc = self.nc
        for e in ENGS:
            assert not self.pending_nobump[e], e
        print("sched sizes", self.simulate(), "nsems", len(self.semnames))
        from contextlib import ExitStack
        with ExitStack() as st:
            sems = {n: st.enter_context(nc.semaphore(n)) for n in self.semnames}
            block = st.enter_context(nc.Block())
            handles = {"pe": block.tensor, "act": block.scalar, "dve": block.vector,
                       "pool": block.gpsimd, "sp": block.sync}

            def mk(ename):
                def body(e):
                    for fn, waits, inc in self.q[ename]:
                        for s, v in waits:
                            e.wait_ge(sems[s], v)
                        if fn is None:
                            continue
                        ins = fn(e)
                        if inc is not None:
                            ins.then_inc(sems[inc[0]], inc[1])
                return body
            for ename in ENGS:
                handles[ename](mk(ename))


class Arena:
    def __init__(self, big, nbytes):
        self.big = big
        self.nbytes = nbytes
        self.off = 0
        self.marks = []

    def alloc(self, nbytes, dtype, shape=None):
        rb = (nbytes + 63) // 64 * 64
        assert self.off + rb <= self.nbytes, (self.off, rb, self.nbytes)
        a = self.big[:, self.off // 2:(self.off + nbytes) // 2]
        self.off += rb
        if dtype is not BF16:
            a = a.bitcast(dtype)
        return a

    def mark(self):
        self.marks.append(self.off)

    def release(self):
        self.off = self.marks.pop()


def v3(ap, a):
    return ap.rearrange("p (a b) -> p a b", a=a)


class Builder:
    def __init__(self, stage):
        self.stage = stage
        self.nc = bass.Bass("TRN2", target_bir_lowering=False)
        self.S = Sched(self.nc)
        self.inputs = {}
        self.outputs = {}
        self.psum_rr = 0
        self.uid = 0
        self.dbgtoks = []

    def din(self, name, shape, dtype=F32):
        t = self.nc.dram_tensor(name, list(shape), dtype, kind="ExternalInput").ap()
        self.inputs[name] = t
        return t

    def dout(self, name, shape, dtype=F32):
        t = self.nc.dram_tensor(name, list(shape), dtype, kind="ExternalOutput").ap()
        self.outputs[name] = t
        return t

    def newid(self, p="r"):
        self.uid += 1
        return "%s%d" % (p, self.uid)

    def dbg(self, name, ap, shape, dtype, reads):
        if "dump" not in SKIP:
            return
        o = self.dout("dbg_" + name, shape, dtype)
        self.dbgtoks.append(self.S.dma("sp", lambda e: e.dma_start(out=o, in_=ap), "dbg_" + name, reads, ["dbg_" + name]))

    def ps(self, dtype=F32):
        b = self.psum_rr
        self.psum_rr = (self.psum_rr + 1) % 8
        ap = self.psum[:, b * 512:(b + 1) * 512]
        if dtype is BF16:
            ap = ap.bitcast(BF16)
        return ap, ("ps", b)

    def mm(self, out, lhsT, rhs, start, stop, reads, writes, bump=None):
        if bump is None:
            bump = stop
        self.S.op("pe", lambda e: e.matmul(out, lhsT, rhs, start=start, stop=stop), reads, writes, bump=bump)

    def tr(self, out, in_, ident, reads, writes, bump=True):
        self.S.op("pe", lambda e: e.transpose(out, in_, ident), reads, writes, bump=bump)

    def act(self, out, in_, func, reads, writes, bias=None, scale=None, accum_out=None):
        kw = {}
        if bias is not None:
            kw["bias"] = bias
        if scale is not None:
            kw["scale"] = scale
        if accum_out is not None:
            kw["accum_out"] = accum_out
        self.S.op("act", lambda e: e.activation(out, in_, func, **kw), reads, writes)

    def stt(self, eng, out, in0, scalar, in1, op0, op1, reads, writes):
        self.S.op(eng, lambda e: e.scalar_tensor_tensor(out, in0, scalar, in1, op0, op1), reads, writes)

    def tt(self, eng, out, in0, in1, op, reads, writes):
        self.S.op(eng, lambda e: e.tensor_tensor(out, in0, in1, op), reads, writes)

    def ts(self, eng, out, in0, s1, s2, op0, op1, reads, writes):
        if op1 is None:
            self.S.op(eng, lambda e: e.tensor_scalar(out, in0, s1, None, op0), reads, writes)
        else:
            self.S.op(eng, lambda e: e.tensor_scalar(out, in0, s1, s2, op0, op1), reads, writes)

    def cp(self, eng, out, in_, reads, writes):
        if eng == "act":
            self.S.op("act", lambda e: e.activation(out, in_, AF.Copy), reads, writes)
        else:
            self.S.op(eng, lambda e: e.tensor_copy(out, in_), reads, writes)

    def wload(self, dst, src, sem, reads, writes, chunk=4096):
        self.S.dma("pool", lambda e: e.dma_start(out=dst, in_=src, max_dma_last_dim=chunk), sem, reads, writes)

    def load(self, dst, src, sem, reads, writes, eng="sp"):
        self.S.dma(eng, lambda e: e.dma_start(out=dst, in_=src), sem, reads, writes)

    def build(self):
        nc = self.nc
        from contextlib import ExitStack
        with ExitStack() as st:
            SB_BYTES = 204 * 1024
            big = st.enter_context(nc.sbuf_tensor("big", [128, SB_BYTES // 2], BF16))
            self.psum = st.enter_context(nc.psum_tensor("psum", [128, 8 * 512], F32))
            self.A = Arena(big, SB_BYTES)
            self.program()
            self.S.emit()
        return nc

    def consts(self):
        A = self.A
        self.ident = A.alloc(128 * 2, BF16)
        self.ones = A.alloc(128 * 2, BF16)
        self.maskT = A.alloc(128 * 4, F32)
        self.gcols = A.alloc(7 * 8 * 4, F32)
        self.epsc = A.alloc(64, F32)
        d_ident = self.din("ident", [128, 128], BF16)
        d_ones = self.din("ones", [128, 128], BF16)
        d_mask = self.din("maskT", [128, 128])
        d_g = self.din("gcols", [128, 56])
        d_eps = self.din("epsc", [128, 16])
        self.load(self.ident, d_ident, "c_ld", [], ["ident"])
        self.load(self.ones, d_ones, "c_ld", [], ["ones"])
        self.load(self.maskT, d_mask, "c_ld", [], ["maskT"])
        self.load(self.gcols, d_g, "c_ld", [], ["gcols"])
        self.load(self.epsc, d_eps, "c_ld", [], ["epsc"])
        self.S.dma_group_end("c_ld", ["ident", "ones", "maskT", "gcols", "epsc"])

    def gcol(self, which, k):
        i = which * 8 + k
        return self.gcols[:, i:i + 1]

    def norm_group(self, g, which, xn_view, xn_res, sq, rstd, sqres=("sq",)):
        t0, t1 = GROUPS[g]
        n = t1 - t0
        sqres = list(sqres)
        hres = [("h", k, g) for k in range(8)]
        sqv = v3(sq, 8)[:, :, :n]
        self.act(sqv, self.hT[:, :, t0:t1], AF.Square, hres, sqres)
        ps, pr = self.ps()
        for k in range(8):
            self.mm(ps[:, :n], self.ones, v3(sq, 8)[:, k, :n], k == 0, k == 7, sqres + ["ones"], [pr])
        self.act(rstd[:, :n], ps[:, :n], AF.Sqrt, [pr, "epsc"], ["rstd"], bias=self.epsc[:, 0:1], scale=1.0 / D)
        self.S.op("dve", lambda e: e.reciprocal(rstd[:, :n], rstd[:, :n]), ["rstd"], ["rstd"])
        for k in range(8):
            eng = "dve"
            self.stt(eng, xn_view[:, k, :n], self.hT[:, k, t0:t1], self.gcol(which, k), rstd[:, :n],
                     ALU.mult, ALU.mult, [("h", k, g), "rstd", "gcols"], [(xn_res, k)])

    def ffn(self, idx, which_norm, w1_d, w2_d):
        A = self.A
        A.mark()
        xn = A.alloc(8 * NT * 2, BF16)
        xn3 = v3(xn, 8)
        actb = A.alloc(11 * NT * 2, BF16)
        act3 = v3(actb, 11)
        w1s = [A.alloc(8 * 256 * 2, BF16) for _ in range(3)]
        w2 = A.alloc(11 * 1024 * 2, BF16)
        w23 = v3(w2, 11)
        rstd = A.alloc(512 * 4, F32)
        stmp = [A.alloc(512 * 4, F32) for _ in range(2)]
        sq = A.alloc(8 * 512 * 2, BF16)
        tag = "f%d" % idx

        def load_w1(j):
            slot = j % 3
            self.wload(w1s[slot], w1_d[j].rearrange("p k c -> p (k c)"), "w1s%d" % slot, [], [("w1", slot)])

        load_w1(0)
        load_w1(1)
        for jj in range(11):
            self.wload(w23[:, jj, :], w2_d[jj], "w2s", [], [("w2", jj)])
        self.S.dma_group_end("w2s", [("w2", jj) for jj in range(11)])
        for g in range(5):
            t0, t1 = GROUPS[g]
            if "norm" in SKIP:
                continue
            self.norm_group(g, which_norm, xn3[:, :, t0:t1], (tag + "xn", g), sq, rstd)
        cnt = 0
        for half in range(2):
            if half == 1:
                for jj in range(11):
                    self.wload(w23[:, jj, :], w2_d[11 + jj], "w2s", [], [("w2", jj)])
                self.S.dma_group_end("w2s", [("w2", jj) for jj in range(11)])
            for jj in range(11):
                j = half * 11 + jj
                slot = j % 3
                w13 = v3(w1s[slot], 8)
                for g in range(5):
                    if "p1" in SKIP:
                        continue
                    t0, t1 = GROUPS[g]
                    n = t1 - t0
                    pg, rg = self.ps()
                    pu, ru = self.ps()
                    xr = [((tag + "xn", g), k) for k in range(8)]
                    for k in range(8):
                        self.mm(pg[:, :n], w13[:, k, 0:128], xn3[:, k, t0:t1], k == 0, k == 7,
                                [("w1", slot), xr[k]], [rg])
                    for k in range(8):
                        self.mm(pu[:, :n], w13[:, k, 128:256], xn3[:, k, t0:t1], k == 0, k == 7,
                                [("w1", slot), xr[k]], [ru])
                    tmp = stmp[cnt % 2]
                    tr_ = ("stmp", cnt % 2)
                    cnt += 1
                    self.act(tmp[:, :n], pg[:, :n], AF.Silu, [rg], [tr_])
                    self.tt("dve", act3[:, jj, t0:t1], pu[:, :n], tmp[:, :n], ALU.mult, [ru, tr_], [("act", jj, g)])
                if j + 2 < NFT:
                    load_w1(j + 2)
            for g in range(5):
                if "p2" in SKIP:
                    continue
                t0, t1 = GROUPS[g]
                n = t1 - t0
                for dt in range(8):
                    po, ro = self.ps()
                    for jj in range(11):
                        self.mm(po[:, :n], w23[:, jj, dt * 128:(dt + 1) * 128], act3[:, jj, t0:t1], jj == 0, jj == 10,
                                [("w2", jj), ("act", jj, g)], [ro])
                    self.stt("dve", self.hT[:, dt, t0:t1], po[:, :n], 0.5, self.hT[:, dt, t0:t1], ALU.mult, ALU.add,
                             [ro, ("h", dt, g)], [("h", dt, g)])
        self.S.barrier()
        A.release()


    def mixer(self, kind, passB, which_norm, d):
        A = self.A
        A.mark()
        ret = kind == "ret"
        H = 4
        ndt = 2 if ret else 1
        dk = 128 * ndt
        dv = 512 if ret else 256
        nqt = H * ndt
        nvt = dv // 128
        nft = H * nvt
        tg = kind + ("B" if passB else "A")
        gam = [1.0 - 2.0 ** (-5.0 - h) for h in range(H)]
        hn = A.alloc(8 * 512 * 2, BF16); hn3 = v3(hn, 8)
        qT = A.alloc(nqt * 512 * 2, BF16); qT3 = v3(qT, nqt)
        kT = A.alloc(nqt * 512 * 2, BF16); kT3 = v3(kT, nqt)
        kTM = A.alloc(4 * H * dk * 2, BF16); kTM3 = v3(kTM, 4)
        vTM = A.alloc(4 * dv * 2, BF16); vTM3 = v3(vTM, 4)
        gw = A.alloc(4 * dv * 2, BF16); gw3 = v3(gw, 4)
        yTM = A.alloc(4 * dv * 2, BF16); yTM3 = v3(yTM, 4)
        scr = A.alloc(16 * 512 * 2, BF16)
        yT3 = v3(scr[:, :nft * 512], nft)
        sq = scr[:, :8 * 512]
        Ltmp = scr.bitcast(F32)[:, :nqt * dv]
        Ltmp3 = v3(Ltmp, nqt)
        X = A.alloc(nqt * dv * 4, F32); X3 = v3(X, nqt)
        Sbf = A.alloc(nqt * dv * 2, BF16); Sbf3 = v3(Sbf, nqt)
        wm = [A.alloc(8 * 512 * 2, BF16) for _ in range(3)]
        wo = [A.alloc(nft * 128 * 2, BF16) for _ in range(2)]
        rstd = A.alloc(512 * 4, F32)
        tmps = [A.alloc(512 * 4, F32) for _ in range(4)]
        gtmp = tmps[2]
        junk = tmps[3]
        scr_res = [(tg + "yT", h_) for h_ in range(H)]
        coef_sb = A.alloc(12 * 4, F32)
        PTb = A.alloc(128 * 2, BF16)
        small = A.alloc(64 * 4, F32)
        ssq = small[:, 0:1]
        rso = small[:, 1:2]
        hncols = A.alloc(16 * 4, F32)
        self.load(hncols[:, :nft], d["hncols"], tg + "_c", [], [tg + "hnB"])
        if ret:
            tabs = A.alloc(4 * 512 * 4, F32); tabs3 = v3(tabs, 4)
        else:
            wz = A.alloc(8 * 16 * 2, BF16); wz3 = v3(wz, 8)
            wgt = A.alloc(512 * 2, BF16)
            gcst = A.alloc(16 * 4, F32)
            rmask = A.alloc(512 * 4, F32)
            zTb = A.alloc(512 * 2, BF16)
            bcum = A.alloc(H * 512 * 4, F32); bcum3 = v3(bcum, H)
            Eq = A.alloc(H * 512 * 4, F32); Eq3 = v3(Eq, H)
            Ek = A.alloc(H * 512 * 4, F32); Ek3 = v3(Ek, H)
            dcols = A.alloc(H * 20 * 4, F32); dcols3 = v3(dcols, H)
            Bacc = A.alloc(4 * 4, F32)
            self.wload(wz, d["wz"].rearrange("p k c -> p (k c)"), tg + "_wz", [], [tg + "wz"])
            self.wload(wgt[0:16, :], d["wgate"], tg + "_wg", [], [tg + "wgt"])
            self.load(gcst, d["gcst"], tg + "_c2", [], [tg + "gcst"])
            self.ts("dve", gcst[:, 0:4], gcst[:, 0:4], -1.0, None, ALU.mult, None, [tg + "gcst"], [tg + "gcst"])
            self.load(rmask, d["rmask"], tg + "_c3", [], [tg + "rmask"])
            self.S.op("dve", lambda e: e.memset(Bacc, 0.0), [], [tg + "Bacc"])
        self.S.op("dve", lambda e: e.memset(X, 0.0), [], [(tg + "X", j) for j in range(nqt)])
        self.S.op("pool", lambda e: e.memset(Sbf, 0.0), [], [(tg + "S", j) for j in range(nqt)])
        if passB:
            if ret:
                self.load(coef_sb, d["coef"], tg + "_cf", [], [tg + "coef"])
            else:
                Bsb = A.alloc(12 * 4, F32)
                sf = A.alloc(12 * 4, F32)
                cum = A.alloc(12 * 4, F32)
                self.load(v3(Bsb, 3), d["Bg"].rearrange("r p h -> p r h"), tg + "_cf", [], [tg + "Bsb"])
                self.load(sf, d["selflag"], tg + "_cf2", [], [tg + "sf"])
                for r in range(3):
                    cr = cum[:, r * 4:(r + 1) * 4]
                    self.ts("dve", cr, Bsb[:, 0:4], sf[:, r * 3:r * 3 + 1], None, ALU.mult, None, [tg + "Bsb", tg + "sf"], [tg + "cum"])
                    for r2 in (1, 2):
                        self.stt("dve", cr, Bsb[:, r2 * 4:(r2 + 1) * 4], sf[:, r * 3 + r2:r * 3 + r2 + 1], cr, ALU.mult, ALU.add,
                                 [tg + "Bsb", tg + "sf", tg + "cum"], [tg + "cum"])
                self.act(coef_sb, cum, AF.Exp, [tg + "cum"], [tg + "coef"])
                for r in range(3):
                    self.ts("dve", coef_sb[:, r * 4:(r + 1) * 4], coef_sb[:, r * 4:(r + 1) * 4], sf[:, 9 + r:10 + r], None,
                            ALU.mult, None, [tg + "coef", tg + "sf"], [tg + "coef"])
        dX = [1.0] * H
        win = d["win"]
        def head_slices(h):
            if ret:
                return 4 + h, 8 + h
            return 2 + h // 2, 4 + h // 2
        seq = []
        for g_ in range(5):
            for h_ in range(H):
                for qk_ in (("q", "k") if passB else ("k",)):
                    base_ = 0 if qk_ == "q" else nqt
                    for dt_ in range(ndt):
                        seq.append((base_ + h_ * ndt + dt_) // 4)
            for h_ in range(H):
                vs_, gs_ = head_slices(h_)
                seq.append(vs_)
                if passB:
                    seq.append(gs_)
        useq = [seq[0]]
        for s_ in seq[1:]:
            if s_ != useq[-1] and not (len(useq) >= 2 and s_ == useq[-2]):
                useq.append(s_)
        wstate = {"p": -1, "issued": -1}

        def wissue(p):
            s = p % 3
            self.wload(wm[s], win[useq[p]].rearrange("p k c -> p (k c)"), tg + "_wm%d" % s, [], [(tg + "wm", s)])
            wstate["issued"] = p

        def wslice(i):
            p = wstate["p"]
            for back in (0, 1):
                if p - back >= 0 and useq[p - back] == i:
                    s = (p - back) % 3
                    return v3(wm[s], 8), (tg + "wm", s)
            p += 1
            assert useq[p] == i, (useq[p], i, p)
            wstate["p"] = p
            if wstate["issued"] < p:
                wissue(p)
            if p + 1 < len(useq) and wstate["issued"] < p + 1:
                wissue(p + 1)
            s = p % 3
            return v3(wm[s], 8), (tg + "wm", s)

        chunk_global = 0
        for g in range(5):
            t0, t1 = GROUPS[g]
            n = t1 - t0
            C = 16 if g == 0 else 128
            nch = n // C
            self.norm_group(g, which_norm, hn3[:, :, :n], tg + "hn", sq[:, :], rstd, sqres=scr_res)
            hres = [(tg + "hn", k) for k in range(8)]
            if not ret:
                pz, rz = self.ps()
                for k in range(8):
                    self.mm(pz[:16, :n], wz3[:, k, :], hn3[:, k, :n], k == 0, k == 7, [tg + "wz", hres[k]], [rz])
                self.cp("act", zTb[0:16, :n], pz[:16, :n], [rz], [tg + "zT"])
                for h in range(H):
                    pl, rl = self.ps()
                    self.mm(pl[:, :n], wgt[0:16, h * 128:(h + 1) * 128], zTb[0:16, :n], True, True, [tg + "wgt", tg + "zT"], [rl])
                    self.act(tmps[0][:, :n], pl[:, :n], AF.Exp, [rl, tg + "gcst"], ["tmp0"], bias=gcst[:, h:h + 1], scale=-1.0)
                    self.act(tmps[1][:, :n], tmps[0][:, :n], AF.Ln, ["tmp0", tg + "gcst"], ["tmp1"], bias=gcst[:, 6:7], scale=1.0)
                    rm = rmask[:, 1:17] if g == 0 else rmask[:, :n]
                    self.S.op("dve", (lambda o=bcum3[:, h, :n], a=rm, b=tmps[1][:, :n]:
                                      (lambda e: e.tensor_tensor_scan(o, a, b, 0.0, ALU.mult, ALU.add)))(),
                              ["tmp1", tg + "rmask"], [(tg + "bc", h)])
                    self.act(Eq3[:, h, :n], bcum3[:, h, :n], AF.Exp, [(tg + "bc", h), tg + "gcst"], [(tg + "Eq", h)],
                             bias=gcst[:, 5:6], scale=-1.0 / 16.0)
                    self.act(Ek3[:, h, :n], bcum3[:, h, :n], AF.Exp, [(tg + "bc", h)], [(tg + "Ek", h)], scale=1.0 / 16.0)
                    if g == 0:
                        self.ts("dve", Ek3[:, h, :n], Ek3[:, h, :n], gcst[:, 4:5], None, ALU.mult, None,
                                [(tg + "Ek", h), tg + "gcst"], [(tg + "Ek", h)])
                    lastcols = bcum3[:, h, C - 1:n:C]
                    self.act(dcols3[:, h, chunk_global:chunk_global + nch], lastcols, AF.Exp, [(tg + "bc", h)],
                             [(tg + "dc", h)], scale=-1.0 / 16.0)
                    if (not passB) and g > 0:
                        for c in range(nch):
                            col = bcum3[:, h, c * C + C - 1:c * C + C]
                            self.tt("dve", Bacc[:, h:h + 1], Bacc[:, h:h + 1], col, ALU.add, [(tg + "bc", h), tg + "Bacc"], [tg + "Bacc"])
            kinds = ("q", "k") if passB else ("k",)
            for h in range(H):
                for qk in kinds:
                    dstT = qT3 if qk == "q" else kT3
                    base = 0 if qk == "q" else nqt
                    pss = []
                    for dt in range(ndt):
                        j = base + h * ndt + dt
                        wv_, wr = wslice(j // 4)
                        pp, pr = self.ps()
                        for k in range(8):
                            self.mm(pp[:, :n], wv_[:, k, (j % 4) * 128:(j % 4 + 1) * 128], hn3[:, k, :n], k == 0, k == 7,
                                    [wr, hres[k]], [pr])
                        pss.append((pp, pr))
                    dres = [(tg + qk + "T", h * ndt + dt) for dt in range(ndt)]
                    if ret:
                        if qk == kinds[0]:
                            self.S.dma("sp", (lambda h=h, t0=t0, t1=t1, n=n: (lambda e: e.dma_start(
                                out=tabs3[:, :, :n], in_=d["rot"][h, :, :, t0:t1].rearrange("a p t -> p a t"))))(),
                                tg + "_tab", [], [tg + "tabs"])
                        ci, si = (0, 1) if qk == "q" else (2, 3)
                        (pa, ra), (pb, rb) = pss
                        cT = tabs3[:, ci, :n]
                        sT = tabs3[:, si, :n]
                        tb = tg + "tabs"
                        self.tt("dve", tmps[0][:, :n], pa[:, :n], cT, ALU.mult, [ra, tb], ["tmp0"])
                        self.tt("dve", tmps[1][:, :n], pb[:, :n], sT, ALU.mult, [rb, tb], ["tmp1"])
                        self.tt("pool", dstT[:, h * 2, :n], tmps[0][:, :n], tmps[1][:, :n], ALU.subtract, ["tmp0", "tmp1"], [dres[0]])
                        self.tt("dve", tmps[2][:, :n], pa[:, :n], sT, ALU.mult, [ra, tb], ["tmp2"])
                        self.tt("dve", tmps[3][:, :n], pb[:, :n], cT, ALU.mult, [rb, tb], ["tmp3"])
                        self.tt("pool", dstT[:, h * 2 + 1, :n], tmps[2][:, :n], tmps[3][:, :n], ALU.add, ["tmp2", "tmp3"], [dres[1]])
                    else:
                        E3 = Eq3 if qk == "q" else Ek3
                        er = (tg + ("Eq" if qk == "q" else "Ek"), h)
                        pp, pr = pss[0]
                        self.tt("dve", dstT[:, h, :n], pp[:, :n], E3[:, h, :n], ALU.mult, [pr, er], [dres[0]])
            if passB and (not ret) and g == 1:
                self.dbg("qT", qT, [128, nqt * 512], BF16, [(tg + "qT", j) for j in range(nqt)])
                self.dbg("kT", kT, [128, nqt * 512], BF16, [(tg + "kT", j) for j in range(nqt)])
                self.dbg("Eq", Eq, [128, H * 512], F32, [(tg + "Eq", j) for j in range(H)])
                self.dbg("Ek", Ek, [128, H * 512], F32, [(tg + "Ek", j) for j in range(H)])
                self.dbg("coef", coef_sb, [128, 12], F32, [tg + "coef"])
            for c in range(nch):
                for j0 in range(0, nqt, 4):
                    pt, prt = self.ps(BF16)
                    for jj in range(4):
                        j = j0 + jj
                        self.tr(pt[:C, jj * 128:(jj + 1) * 128], kT3[:, j, c * C:(c + 1) * C], self.ident,
                                [(tg + "kT", j), "ident"], [prt], bump=(jj == 3))
                    self.cp("act", kTM3[:C, c, j0 * 128:(j0 + 4) * 128], pt[:C, :512], [prt], [(tg + "kTM", c)])
            if passB and g == 1:
              for h in range(H):
                  for dt in range(ndt):
                      j = h * ndt + dt
                      self.ts("dve", X3[:, j, :], X3[:, j, :], dX[h], None, ALU.mult, None, [(tg + "X", j)], [(tg + "X", j)])
                  for r in range(3):
                      self.load(Ltmp3[:, h * ndt:(h + 1) * ndt, :], d["Lg"][r][:, h * ndt:(h + 1) * ndt, :], tg + "_lt",
                                [], scr_res)
                      for dt in range(ndt):
                          j = h * ndt + dt
                          cf = coef_sb[:, r * 4 + h:r * 4 + h + 1]
                          self.stt("dve", X3[:, j, :], Ltmp3[:, j, :], cf, X3[:, j, :], ALU.mult, ALU.add,
                                   scr_res + [(tg + "X", j), tg + "coef"], [(tg + "X", j)])
                  for dt in range(ndt):
                      j = h * ndt + dt
                      self.cp("act", Sbf3[:, j, :], X3[:, j, :], [(tg + "X", j)], [(tg + "S", j)])
                  dX[h] = 1.0
            for h in range(H):
                if ret:
                    vsl, voff = 4 + h, 0
                    gsl, goff = 8 + h, 0
                else:
                    vsl, voff = 2 + h // 2, (h % 2) * 256
                    gsl, goff = 4 + h // 2, (h % 2) * 256
                wv_, wr = wslice(vsl)
                for c in range(nch):
                    pv, prv = self.ps()
                    for k in range(8):
                        self.mm(pv[:C, :dv], hn3[:, k, c * C:(c + 1) * C], wv_[:, k, voff:voff + dv], k == 0, k == 7,
                                [wr, hres[k]], [prv])
                    self.cp("act", vTM3[:C, c, :], pv[:C, :dv], [prv], [(tg + "v", c)])
                if passB:
                    wg_, wgr = wslice(gsl)
                    for c in range(nch):
                        pg_, prg = self.ps()
                        for k in range(8):
                            self.mm(pg_[:C, :dv], hn3[:, k, c * C:(c + 1) * C], wg_[:, k, goff:goff + dv], k == 0, k == 7,
                                    [wgr, hres[k]], [prg])
                        self.act(gw3[:C, c, :], pg_[:C, :dv], AF.Silu, [prg], [(tg + "gw", c)])
                for c in range(nch):
                    cg = chunk_global + c
                    c0, c1 = c * C, (c + 1) * C
                    last_chunk = (g == 4 and c == nch - 1)
                    if passB:
                        psc, rsc = self.ps()
                        for dt in range(ndt):
                            j = h * ndt + dt
                            self.mm(psc[:C, :C], kT3[:, j, c0:c1], qT3[:, j, c0:c1], dt == 0, dt == ndt - 1,
                                    [(tg + "kT", j), (tg + "qT", j)], [rsc])
                        self.tt("dve", PTb[:C, :C], psc[:C, :C], self.maskT[:C, :C], ALU.mult, [rsc, "maskT"], [tg + "PT"])
                        po, ro = self.ps()
                        self.mm(po[:C, :dv], PTb[:C, :C], vTM3[:C, c, :], True, False, [tg + "PT", (tg + "v", c)], [ro], bump=False)
                        for dt in range(ndt):
                            j = h * ndt + dt
                            self.mm(po[:C, :dv], qT3[:, j, c0:c1], Sbf3[:, j, :], False, dt == ndt - 1,
                                    [(tg + "qT", j), (tg + "S", j)], [ro])
                    if not (passB and last_chunk):
                        dc = (gam[h] ** C) if ret else dcols3[:, h, cg:cg + 1]
                        dcr = [] if ret else [(tg + "dc", h)]
                        for dt in range(ndt):
                            j = h * ndt + dt
                            pu, ru = self.ps()
                            self.mm(pu[:, :dv], kTM3[:C, c, j * 128:(j + 1) * 128], vTM3[:C, c, :], True, True,
                                    [(tg + "kTM", c), (tg + "v", c)], [ru])
                            self.stt("dve", X3[:, j, :], X3[:, j, :], dX[h], pu[:, :dv], ALU.mult, ALU.add,
                                     [(tg + "X", j), ru] + dcr, [(tg + "X", j)])
                            if passB:
                                self.ts("dve", Sbf3[:, j, :], X3[:, j, :], dc, None, ALU.mult, None,
                                        [(tg + "X", j)] + dcr, [(tg + "S", j)])
                        dX[h] = dc
                    if passB:
                        self.act(junk[:C, :dv], po[:C, :dv], AF.Square, [ro], ["tmp3", tg + "ssq"], accum_out=ssq[:C, :])
                        self.act(rso[:C, :], ssq[:C, :], AF.Sqrt, [tg + "ssq", "epsc"], [tg + "rso"], bias=self.epsc[:C, 0:1], scale=1.0 / dv)
                        self.S.op("dve", (lambda a=rso[:C, :]: (lambda e: e.reciprocal(a, a)))(), [tg + "rso"], [tg + "rso"])
                        self.stt("dve", yTM3[:C, c, :], po[:C, :dv], rso[:C, :], gw3[:C, c, :], ALU.mult, ALU.mult,
                                 [ro, tg + "rso", (tg + "gw", c)], [(tg + "y", c)])
                if passB:
                    for c in range(nch):
                        pt, prt = self.ps(BF16)
                        for vt in range(nvt):
                            self.tr(pt[:, vt * C:(vt + 1) * C], yTM3[:C, c, vt * 128:(vt + 1) * 128], self.ident[:C, :C],
                                    [(tg + "y", c), "ident"], [prt], bump=(vt == nvt - 1))
                        for vt in range(nvt):
                            ft = h * nvt + vt
                            self.ts("dve", yT3[:, ft, c * C:(c + 1) * C], pt[:, vt * C:(vt + 1) * C], hncols[:, ft:ft + 1], None,
                                    ALU.mult, None, [prt, tg + "hnB"], [(tg + "yT", h)])
            if passB and (not ret) and g == 1:
                self.dbg("yT", scr[:, :nft * 512], [128, nft * 512], BF16, scr_res)
                self.dbg("X", X, [128, nqt * dv], F32, [(tg + "X", j) for j in range(nqt)])
                self.dbg("v3", vTM, [128, 4 * dv], BF16, [(tg + "v", c) for c in range(4)])
                self.dbg("gw3", gw, [128, 4 * dv], BF16, [(tg + "gw", c) for c in range(4)])
                self.dbg("yTM3", yTM, [128, 4 * dv], BF16, [(tg + "y", c) for c in range(4)])
            chunk_global += nch
            if passB:
                for dt in range(8):
                    s = dt % 2
                    if dt == 0:
                        self.wload(wo[0], d["wout"][0].rearrange("p k c -> p (k c)"), tg + "_wo0", [], [(tg + "wo", 0)])
                    if dt + 1 < 8:
                        self.wload(wo[(dt + 1) % 2], d["wout"][dt + 1].rearrange("p k c -> p (k c)"), tg + "_wo%d" % ((dt + 1) % 2),
                                   [], [(tg + "wo", (dt + 1) % 2)])
                    wo3 = v3(wo[s], nft)
                    pm, rm_ = self.ps()
                    for ft in range(nft):
                        self.mm(pm[:, :n], wo3[:, ft, :], yT3[:, ft, :n], ft == 0, ft == nft - 1,
                                [(tg + "wo", s), (tg + "yT", ft // nvt)], [rm_])
                    self.stt("dve", self.hT[:, dt, t0:t1], pm[:, :n], 1.0, self.hT[:, dt, t0:t1], ALU.mult, ALU.add,
                             [rm_, ("h", dt, g)], [("h", dt, g)])
        toks = []
        if not passB:
            for h in range(H):
                for dt in range(ndt):
                    j = h * ndt + dt
                    dcr = [] if ret else [(tg + "dc", h)]
                    self.ts("dve", X3[:, j, :], X3[:, j, :], dX[h], None, ALU.mult, None, [(tg + "X", j)] + dcr, [(tg + "X", j)])
            toks.append(self.S.dma("sp", lambda e: e.dma_start(out=d["Lout"], in_=X), tg + "_lo",
                                   [(tg + "X", j) for j in range(nqt)], [tg + "Lout"]))
            if not ret:
                self.ts("dve", Bacc, Bacc, -1.0 / 16.0, None, ALU.mult, None, [tg + "Bacc"], [tg + "Bacc"])
                toks.append(self.S.dma("sp", lambda e: e.dma_start(out=d["Bout"], in_=Bacc), tg + "_bo", [tg + "Bacc"], [tg + "Bout"]))
        self.S.barrier()
        A.release()
        return toks

    def final_norm(self, which, out_d):
        A = self.A
        A.mark()
        sq = A.alloc(8 * 512 * 2, BF16)
        rstd = A.alloc(512 * 4, F32)
        ob = [A.alloc(8 * 512 * 4, F32) for _ in range(2)]
        toks = []
        for g in range(1, 5):
            t0, t1 = GROUPS[g]
            o3 = v3(ob[g % 2], 8)
            self.norm_group(g, which, o3, ("fo", g % 2), sq, rstd)
            tok = self.S.dma("sp", (lambda o3=o3, t0=t0, t1=t1: (lambda e: e.dma_start(out=out_d[:, :, t0 - 16:t1 - 16], in_=o3)))(),
                             "o_st%d" % (g % 2), [(("fo", g % 2), k) for k in range(8)], [("outd", g)])
            toks.append(tok)
        self.S.final_wait("sp", toks)
        A.release()

    def load_h(self, src_d):
        for g in range(5):
            t0, t1 = GROUPS[g]
            self.load(self.hT[:, :, t0:t1], src_d[:, :, t0:t1], "h_ld%d" % g, [], [("h", k, g) for k in range(8)])

    def store_h(self, dst_d):
        toks = []
        for g in range(5):
            t0, t1 = GROUPS[g]
            tok = self.S.dma("sp", (lambda t0=t0, t1=t1: (lambda e: e.dma_start(out=dst_d[:, :, t0:t1], in_=self.hT[:, :, t0:t1])))(),
                             "h_st", [("h", k, g) for k in range(8)], [("hout", g)])
            toks.append(tok)
        return toks

    def program(self):
        A = self.A
        hT = A.alloc(8 * NT * 4, F32)
        self.hT = v3(hT, 8)
        self.consts()
        st = self.stage
        if st == "ffn_only":
            xT = self.din("xT", [128, 8, NT])
            w1 = self.din("w1_0", [NFT, 128, 8, 256])
            w2 = self.din("w2_0", [NFT, 128, 1024])
            hout = self.dout("hout", [128, 8, NT])
            self.load_h(xT)
            self.ffn(0, 0, w1, w2)
            toks = self.store_h(hout)
            self.S.final_wait("sp", toks)
            return

        def ffn_in(i):
            return self.din("w1_%d" % i, [NFT, 128, 8, 256]), self.din("w2_%d" % i, [NFT, 128, 1024])

        def ret_in(passB):
            d = {"win": self.din("ret_win", [12, 128, 8, 512]), "rot": self.din("rot", [4, 4, 128, NT]),
                 "hncols": self.din("ret_hnc", [128, 16])}
            if passB:
                d["wout"] = self.din("ret_wout", [8, 128, 16, 128])
                d["Lg"] = self.din("ret_Lg", [3, 128, 8, 512])
                d["coef"] = self.din("ret_coef", [128, 12])
            else:
                d["Lout"] = self.dout("ret_L", [128, 8 * 512])
            return d

        def gla_in(passB):
            d = {"win": self.din("gla_win", [6, 128, 8, 512]), "wz": self.din("gla_wz", [128, 8, 16]),
                 "wgate": self.din("gla_wgate", [16, 512]), "gcst": self.din("gla_gcst", [128, 16]),
                 "rmask": self.din("gla_rmask", [128, 512]), "hncols": self.din("gla_hnc", [128, 8])}
            if passB:
                d["wout"] = self.din("gla_wout", [8, 128, 8, 128])
                d["Lg"] = self.din("gla_Lg", [3, 128, 4, 256])
                d["Bg"] = self.din("gla_Bg", [3, 128, 4])
                d["selflag"] = self.din("gla_selflag", [128, 12])
            else:
                d["Lout"] = self.dout("gla_L", [128, 4 * 256])
                d["Bout"] = self.dout("gla_B", [128, 4])
            return d

        toks = []
        if st == "l1":
            self.load_h(self.din("xT", [128, 8, NT]))
            self.ffn(0, 0, *ffn_in(0))
            toks += self.mixer("ret", False, 1, ret_in(False))
            toks += self.store_h(self.dout("hout", [128, 8, NT]))
        elif st == "l2":
            self.load_h(self.din("hin", [128, 8, NT]))
            self.mixer("ret", True, 1, ret_in(True))
            self.ffn(1, 2, *ffn_in(1))
            self.ffn(2, 3, *ffn_in(2))
            toks += self.mixer("gla", False, 4, gla_in(False))
            toks += self.store_h(self.dout("hout", [128, 8, NT]))
        elif st == "l3":
            self.load_h(self.din("hin", [128, 8, NT]))
            self.mixer("gla", True, 4, gla_in(True))
            if "dbg3" in SKIP:
                toks += self.store_h(self.dout("hdbg", [128, 8, NT]))
            self.ffn(3, 5, *ffn_in(3))
            self.final_norm(6, self.dout("outT", [128, 8, SEG]))
        self.S.final_wait("sp", toks + self.dbgtoks)


def _fm(arr):
    T = arr.shape[0]
    return np.ascontiguousarray(arr.reshape(T, 8, 128).transpose(2, 1, 0))


def _w1_layout(w_in):
    g = w_in[:, :DFF].reshape(8, 128, NFT, 128)
    u = w_in[:, DFF:].reshape(8, 128, NFT, 128)
    w = np.concatenate([g, u], axis=3)
    return np.ascontiguousarray(w.transpose(2, 1, 0, 3))


def _w2_layout(w_out):
    return np.ascontiguousarray(w_out.reshape(NFT, 128, D))


def _gcols(vecs):
    out = np.zeros((128, 56), np.float32)
    for i, v in enumerate(vecs):
        out[:, i * 8:(i + 1) * 8] = v.reshape(8, 128).T
    return out


def _common_consts(inp):
    ident = np.eye(128, dtype=np.float32).astype(ml_dtypes.bfloat16)
    ones = np.ones((128, 128), np.float32).astype(ml_dtypes.bfloat16)
    maskT = np.triu(np.ones((128, 128), np.float32))
    gc = _gcols([inp["norm_ffn1"][0], inp["norm_mix"][0], inp["norm_ffn2"][0],
                 inp["norm_ffn1"][1], inp["norm_mix"][1], inp["norm_ffn2"][1], inp["final_norm"]])
    epsc = np.full((128, 16), EPS, np.float32)
    return {"ident": ident, "ones": ones, "maskT": maskT, "gcols": gc, "epsc": epsc}


def _core_tokens(inp, c):
    b, s = c // 4, c % 4
    return np.concatenate([inp["meta_tokens"], inp["x"][b, s * SEG:(s + 1) * SEG]], axis=0)


def _slices512(w):
    ns = w.shape[1] // 512
    return np.ascontiguousarray(w.reshape(8, 128, ns, 512).transpose(2, 1, 0, 3))


def _wout_layout(w):
    nft = w.shape[0] // 128
    return np.ascontiguousarray(w.reshape(nft, 128, 8, 128).transpose(2, 1, 0, 3))


_DBG = {}
_LG = [np.log1p(-2.0 ** (-5.0 - h)) for h in range(4)]


def _rot_tables(s):
    pos = np.concatenate([np.arange(NMETA), NMETA + SEG * s + np.arange(SEG)]).astype(np.float32)
    iloc = np.concatenate([np.arange(NMETA), np.arange(SEG) % 128]).astype(np.float64)
    inv = (np.float32(1.0) / (np.float32(10000.0) ** np.linspace(0.0, 1.0, 128, dtype=np.float32))).astype(np.float32)
    ang = (pos[:, None] * inv[None, :]).astype(np.float32)
    cos = np.cos(ang).astype(np.float32).astype(np.float64)
    sin = np.sin(ang).astype(np.float32).astype(np.float64)
    rot = np.zeros((4, 4, 128, NT), np.float32)
    for h in range(4):
        gq = np.exp(_LG[h] * (iloc + 1.0))
        gk = np.exp(-_LG[h] * (iloc + 1.0)) * (256.0 ** -0.5)
        if s > 0:
            gk[:NMETA] = 0.0
        rot[h, 0] = (cos * gq[:, None]).T
        rot[h, 1] = (sin * gq[:, None]).T
        rot[h, 2] = (cos * gk[:, None]).T
        rot[h, 3] = (sin * gk[:, None]).T
    return rot


def _core_consts(s):
    coef = np.zeros((128, 12), np.float32)
    sf = np.zeros((128, 12), np.float32)
    for r in range(3):
        for h in range(4):
            if r < s:
                coef[:, r * 4 + h] = np.exp(_LG[h] * SEG * (s - 1 - r))
        for r2 in range(3):
            if r < r2 < s:
                sf[:, r * 3 + r2] = 1.0
        if r < s:
            sf[:, 9 + r] = 1.0
    return coef, sf


def run_stage(stage, in_maps):
    b = Builder(stage)
    nc = b.build()
    maps = []
    for m in in_maps:
        maps.append({k: m[k] for k in b.inputs})
    res = run_bass_kernel_spmd(nc, maps, core_ids=list(range(8)))
    return res.results


def kernel(**inputs):
    inp = {k: np.asarray(v) for k, v in inputs.items()}
    base = _common_consts(inp)
    ffw = [(inp["ffn1_w_in"][0], inp["ffn1_w_out"][0]), (inp["ffn2_w_in"][0], inp["ffn2_w_out"][0]),
           (inp["ffn1_w_in"][1], inp["ffn1_w_out"][1]), (inp["ffn2_w_in"][1], inp["ffn2_w_out"][1])]
    for i, (a, b_) in enumerate(ffw):
        base["w1_%d" % i] = _w1_layout(a)
        base["w2_%d" % i] = _w2_layout(b_)
    base["ret_win"] = _slices512(inp["ret_w_in"][0])
    base["ret_wout"] = _wout_layout(inp["ret_w_out"][0])
    base["ret_hnc"] = np.ascontiguousarray(inp["ret_head_norm"][0].reshape(16, 128).T)
    gw = inp["gla_w_in"][0]
    base["gla_win"] = _slices512(gw[:, :3072])
    base["gla_wz"] = np.ascontiguousarray(gw[:, 3072:3088].reshape(8, 128, 16).transpose(1, 0, 2))
    base["gla_wgate"] = np.ascontiguousarray(inp["gla_w_gate"][0])
    base["gla_wout"] = _wout_layout(inp["gla_w_out"][0])
    base["gla_hnc"] = np.ascontiguousarray(inp["gla_head_norm"][0].reshape(8, 128).T)
    rmask = np.ones((128, 512), np.float32)
    rmask[:, ::128] = 0.0
    base["gla_rmask"] = rmask
    rots = [_rot_tables(s) for s in range(4)]
    maps = []
    for c in range(8):
        b, s = c // 4, c % 4
        m = dict(base)
        m["xT"] = _fm(_core_tokens(inp, c))
        m["rot"] = rots[s]
        coef, sf = _core_consts(s)
        m["ret_coef"] = coef
        m["gla_selflag"] = sf
        gc = np.zeros((128, 16), np.float32)
        gc[:, 0:4] = inp["gla_b_gate"][0].reshape(4, 128).T
        gc[:, 4] = 1.0 if s == 0 else 0.0
        gc[:, 5] = np.log(128.0 ** -0.5)
        gc[:, 6] = 1.0
        m["gla_gcst"] = gc
        maps.append(m)
    r1 = run_stage("l1", maps)
    for c in range(8):
        b = c // 4
        maps[c]["hin"] = r1[c]["hout"]
        maps[c]["ret_Lg"] = np.stack([r1[4 * b + r]["ret_L"].reshape(128, 8, 512) for r in range(3)])
    r2 = run_stage("l2", maps)
    for c in range(8):
        b = c // 4
        maps[c]["hin"] = r2[c]["hout"]
        maps[c]["gla_Lg"] = np.stack([r2[4 * b + r]["gla_L"].reshape(128, 4, 256) for r in range(3)])
        maps[c]["gla_Bg"] = np.stack([r2[4 * b + r]["gla_B"] for r in range(3)])
    r3 = run_stage("l3", maps)
    out = np.zeros((2, 4 * SEG, D), np.float32)
    for c in range(8):
        b, s = c // 4, c % 4
        out[b, s * SEG:(s + 1) * SEG] = r3[c]["outT"].transpose(2, 1, 0).reshape(SEG, D)
    _DBG["r1"], _DBG["r2"] = r1, r2
    return out
```
